# Optimizing a Trainium2 kernel written in Bass

```python
import math
import jax, jax.numpy as jnp
from jax import lax
import numpy as np

D_MODEL = 2048
BATCH = 4
SEQ = 4096
DEPTH = 4

CTX_LEN = 256
GRID_W = 64
Q_BLOCK = 128
ROPE_THETA = 10000.0
NORM_EPS = 1e-5

MLA_HEADS = 8
MLA_Q_LORA = 512
MLA_KV_LORA = 512
MLA_NOPE = 128
MLA_ROPE = 64
MLA_V = 128
MLA_QK = MLA_NOPE + MLA_ROPE

SGU_CHUNK = 128
SGU_GROUPS = 8
SGU_GROUP_W = 128
SGU_W = SGU_GROUPS * SGU_GROUP_W

DIFF_HEADS = 4
DIFF_HEAD_DIM = 128
DIFF_V = 2 * DIFF_HEAD_DIM

N_BRANCH = 3
BRANCH_W = 1024
FFN_HIDDEN = -(-8 * D_MODEL // (3 * 256)) * 256

IN_SIZES = [MLA_Q_LORA, MLA_KV_LORA, MLA_ROPE, 2 * SGU_W,
            DIFF_HEADS * 2 * DIFF_HEAD_DIM, DIFF_HEADS * 2 * DIFF_HEAD_DIM,
            DIFF_HEADS * DIFF_V, N_BRANCH * D_MODEL]
IN_W = int(sum(IN_SIZES))
IN_SPLITS = [int(s) for s in np.cumsum(IN_SIZES)[:-1]]

DEEPNORM_ALPHA = (2 * DEPTH) ** 0.25
DEEPNORM_BETA = (8 * DEPTH) ** -0.25

kernel_name = 'hybrid_mla_sgu_diffattn_deepnorm'


def layer_norm(x, g, b):
    xf = x.astype(jnp.float32)
    mu = jnp.mean(xf, axis=-1, keepdims=True)
    var = jnp.mean(jnp.square(xf - mu), axis=-1, keepdims=True)
    return ((xf - mu) * lax.rsqrt(var + NORM_EPS)).astype(x.dtype) * g + b


def rms_norm(x, g):
    xf = x.astype(jnp.float32)
    return (xf * lax.rsqrt(jnp.mean(xf * xf, axis=-1, keepdims=True) + NORM_EPS)).astype(x.dtype) * g


def axial_rope_tables(rows, dim, dtype):
    r = jnp.repeat(jnp.arange(rows, dtype=jnp.float32), GRID_W)
    col = jnp.tile(jnp.arange(GRID_W, dtype=jnp.float32), rows)
    quarter = dim // 4
    inv = ROPE_THETA ** (-jnp.arange(quarter, dtype=jnp.float32) / quarter)
    ar = r[:, None] * inv
    ac = col[:, None] * inv
    ang = jnp.concatenate([ar, ar, ac, ac], axis=-1)
    return jnp.cos(ang).astype(dtype), jnp.sin(ang).astype(dtype)


def apply_rope(x, cos, sin):
    xa = x.reshape(x.shape[:-1] + (2, 2, x.shape[-1] // 4))
    rot = jnp.stack([-xa[..., 1, :], xa[..., 0, :]], axis=-2).reshape(x.shape)
    return x * cos + rot * sin


def block_attention(q, k, v):
    B, H, Sq, dk = q.shape
    nb = Sq // Q_BLOCK
    scale = dk ** -0.5
    qb = q.reshape(B, H, nb, Q_BLOCK, dk).transpose(2, 0, 1, 3, 4)

    def one_block(qblk):
        s = jnp.einsum('bhqd,bhkd->bhqk', qblk, k, preferred_element_type=jnp.float32) * scale
        p = jax.nn.softmax(s, axis=-1).astype(v.dtype)
        return jnp.einsum('bhqk,bhkd->bhqd', p, v)

    o = lax.map(one_block, qb)
    return o.transpose(1, 2, 0, 3, 4).reshape(B, H, Sq, v.shape[-1])


def heads_to_tokens(o):
    B, H, S, d = o.shape
    return o.transpose(0, 2, 1, 3).reshape(B, S, H * d)


def mla_qkv(zq, zkv, zkr, q_norm, w_uq, kv_norm, w_ukv, rope):
    B, S, _ = zq.shape
    q = (rms_norm(zq, q_norm) @ w_uq).reshape(B, S, MLA_HEADS, MLA_QK).transpose(0, 2, 1, 3)
    kv = (rms_norm(zkv, kv_norm) @ w_ukv).reshape(B, S, MLA_HEADS, MLA_NOPE + MLA_V).transpose(0, 2, 1, 3)
    q_nope, q_rope = q[..., :MLA_NOPE], q[..., MLA_NOPE:]
    k_nope, v = kv[..., :MLA_NOPE], kv[..., MLA_NOPE:]
    k_rope = zkr[:, None]
    if rope is not None:
        cos, sin = rope
        q_rope = apply_rope(q_rope, cos, sin)
        k_rope = apply_rope(k_rope, cos, sin)
    q = jnp.concatenate([q_nope, q_rope], axis=-1)
    k = jnp.concatenate([k_nope, jnp.broadcast_to(k_rope, (B, MLA_HEADS, S, MLA_ROPE))], axis=-1)
    return q, k, v


def diff_qkv(zq, zk, zv, rope):
    B, S, _ = zq.shape
    q = zq.reshape(B, S, DIFF_HEADS, 2, DIFF_HEAD_DIM).transpose(3, 0, 2, 1, 4)
    k = zk.reshape(B, S, DIFF_HEADS, 2, DIFF_HEAD_DIM).transpose(3, 0, 2, 1, 4)
    if rope is not None:
        cos, sin = rope
        q = apply_rope(q, cos, sin)
        k = apply_rope(k, cos, sin)
    v = zv.reshape(B, S, DIFF_HEADS, DIFF_V).transpose(0, 2, 1, 3)
    return q, k, v


def diff_lambda(lp, lam_init):
    lpf = lp.astype(jnp.float32)
    return jnp.exp(jnp.sum(lpf[0] * lpf[1])) - jnp.exp(jnp.sum(lpf[2] * lpf[3])) + lam_init


def diff_combine(o1, o2, lam, subln_g, lam_init):
    o = rms_norm(o1 - lam * o2, subln_g) * (1.0 - lam_init)
    return heads_to_tokens(o)


def spatial_gating(z, ln_g, ln_b, w_s, b_s):
    B, S, _ = z.shape
    u, v = jnp.split(jax.nn.gelu(z), 2, axis=-1)
    v = layer_norm(v, ln_g, ln_b)
    vc = v.reshape(B, S // SGU_CHUNK, SGU_CHUNK, SGU_GROUPS, SGU_GROUP_W)
    s = jnp.einsum('gpq,bnqgc->bnpgc', w_s, vc) + b_s.T[:, :, None]
    return u * s.reshape(B, S, SGU_W)


def merge_branches(z_gate, ya, ys, yd, w_branch, w_out):
    gates = jax.nn.sigmoid(z_gate).reshape(z_gate.shape[:-1] + (N_BRANCH, D_MODEL))
    y = jnp.stack([ya, ys, yd], axis=-2)
    proj = jnp.einsum('bsnc,ncd->bsnd', y, w_branch)
    return jnp.sum(gates * proj, axis=-2) @ w_out


def swiglu(h, w_gu, w_down):
    gt, up = jnp.split(h @ w_gu, 2, axis=-1)
    return (jax.nn.silu(gt) * up) @ w_down


def setup_inputs(seed: int = 0) -> dict:
    key = jax.random.key(seed)
    ks = iter(jax.random.split(key, 32))
    L, D = DEPTH, D_MODEL
    beta = DEEPNORM_BETA

    def nrm(shape, s):
        return jax.random.normal(next(ks), shape, jnp.float32) * s

    return {
        'x': nrm((BATCH, SEQ, D), 1.0),
        'c': nrm((BATCH, D), 1.0),
        'ctx': nrm((BATCH, CTX_LEN, D), 1.0),
        'c_ctx': nrm((D,), 1.0),
        'ada_w': nrm((L, D, 6 * D), 0.5 * D ** -0.5),
        'ada_b': nrm((L, 6 * D), 0.02),
        'w_in': nrm((L, D, IN_W), D ** -0.5),
        'mla_q_norm': 1.0 + nrm((L, MLA_Q_LORA), 0.1),
        'mla_w_uq': nrm((L, MLA_Q_LORA, MLA_HEADS * MLA_QK), MLA_Q_LORA ** -0.5),
        'mla_kv_norm': 1.0 + nrm((L, MLA_KV_LORA), 0.1),
        'mla_w_ukv': nrm((L, MLA_KV_LORA, MLA_HEADS * (MLA_NOPE + MLA_V)), MLA_KV_LORA ** -0.5),
        'sgu_ln_g': 1.0 + nrm((L, SGU_W), 0.1),
        'sgu_ln_b': nrm((L, SGU_W), 0.02),
        'sgu_w': nrm((L, SGU_GROUPS, SGU_CHUNK, SGU_CHUNK), 0.5 * SGU_CHUNK ** -0.5),
        'sgu_b': 1.0 + nrm((L, SGU_GROUPS, SGU_CHUNK), 0.1),
        'diff_lam': nrm((L, 4, DIFF_HEAD_DIM), 0.1),
        'diff_subln': 1.0 + nrm((L, DIFF_V), 0.1),
        'w_branch': nrm((L, N_BRANCH, BRANCH_W, D), beta * BRANCH_W ** -0.5),
        'w_out': nrm((L, D, D), beta * D ** -0.5),
        'ln1_g': 1.0 + nrm((L, D), 0.1),
        'ln1_b': nrm((L, D), 0.02),
        'ffn_w_gu': nrm((L, D, 2 * FFN_HIDDEN), D ** -0.5),
        'ffn_w_down': nrm((L, FFN_HIDDEN, D), beta * FFN_HIDDEN ** -0.5),
        'ln2_g': 1.0 + nrm((L, D), 0.1),
        'ln2_b': nrm((L, D), 0.02),
    }


def reference(x, c, ctx, c_ctx, ada_w, ada_b, w_in, mla_q_norm, mla_w_uq, mla_kv_norm, mla_w_ukv,
              sgu_ln_g, sgu_ln_b, sgu_w, sgu_b, diff_lam, diff_subln, w_branch, w_out,
              ln1_g, ln1_b, ffn_w_gu, ffn_w_down, ln2_g, ln2_b):
    n_lat = x.shape[1]
    rows = n_lat // GRID_W
    rope_mla = axial_rope_tables(rows, MLA_ROPE, x.dtype)
    rope_diff = axial_rope_tables(rows, DIFF_HEAD_DIM, x.dtype)
    silu_c = jax.nn.silu(c)
    silu_cc = jax.nn.silu(c_ctx)
    h_lat, h_ctx = x, ctx
    for l in range(DEPTH):
        ctx_out = l < DEPTH - 1
        lam_init = 0.8 - 0.6 * math.exp(-0.3 * l)
        lam = diff_lambda(diff_lam[l], lam_init).astype(x.dtype)
        sh1, sc1, g1, sh2, sc2, g2 = jnp.split((silu_c @ ada_w[l] + ada_b[l])[:, None, :], 6, axis=-1)
        csh1, csc1, cg1, csh2, csc2, cg2 = jnp.split(silu_cc @ ada_w[l] + ada_b[l], 6, axis=-1)

        z_lat = jnp.split((h_lat * (1 + sc1) + sh1) @ w_in[l], IN_SPLITS, axis=-1)
        z_ctx = jnp.split((h_ctx * (1 + csc1) + csh1) @ w_in[l], IN_SPLITS, axis=-1)
        mla_w = (mla_q_norm[l], mla_w_uq[l], mla_kv_norm[l], mla_w_ukv[l])
        qa_l, ka_l, va_l = mla_qkv(z_lat[0], z_lat[1], z_lat[2], *mla_w, rope_mla)
        qa_c, ka_c, va_c = mla_qkv(z_ctx[0], z_ctx[1], z_ctx[2], *mla_w, None)
        qd_l, kd_l, vd_l = diff_qkv(z_lat[4], z_lat[5], z_lat[6], rope_diff)
        qd_c, kd_c, vd_c = diff_qkv(z_ctx[4], z_ctx[5], z_ctx[6], None)

        ya_l = heads_to_tokens(block_attention(qa_l, jnp.concatenate([ka_c, ka_l], axis=2),
                                               jnp.concatenate([va_c, va_l], axis=2)))
        kd_all = jnp.concatenate([kd_c, kd_l], axis=3)
        vd_all = jnp.concatenate([vd_c, vd_l], axis=2)
        yd_l = diff_combine(block_attention(qd_l[0], kd_all[0], vd_all),
                            block_attention(qd_l[1], kd_all[1], vd_all), lam, diff_subln[l], lam_init)
        ys_l = spatial_gating(z_lat[3], sgu_ln_g[l], sgu_ln_b[l], sgu_w[l], sgu_b[l])
        mix_lat = merge_branches(z_lat[7], ya_l, ys_l, yd_l, w_branch[l], w_out[l])

        if ctx_out:
            ya_c = heads_to_tokens(block_attention(qa_c, ka_c, va_c))
            yd_c = diff_combine(block_attention(qd_c[0], kd_c[0], vd_c),
                                block_attention(qd_c[1], kd_c[1], vd_c), lam, diff_subln[l], lam_init)
            ys_c = spatial_gating(z_ctx[3], sgu_ln_g[l], sgu_ln_b[l], sgu_w[l], sgu_b[l])
            mix_ctx = merge_branches(z_ctx[7], ya_c, ys_c, yd_c, w_branch[l], w_out[l])
            h_ctx = layer_norm(DEEPNORM_ALPHA * h_ctx + cg1 * mix_ctx, ln1_g[l], ln1_b[l])
            ff_ctx = swiglu(h_ctx * (1 + csc2) + csh2, ffn_w_gu[l], ffn_w_down[l])
            h_ctx = layer_norm(DEEPNORM_ALPHA * h_ctx + cg2 * ff_ctx, ln2_g[l], ln2_b[l])

        h_lat = layer_norm(DEEPNORM_ALPHA * h_lat + g1 * mix_lat, ln1_g[l], ln1_b[l])
        ff_lat = swiglu(h_lat * (1 + sc2) + sh2, ffn_w_gu[l], ffn_w_down[l])
        h_lat = layer_norm(DEEPNORM_ALPHA * h_lat + g2 * ff_lat, ln2_g[l], ln2_b[l])
    return h_lat
```

```python
import math
from contextlib import ExitStack

import numpy as np
import concourse.bass as bass
import concourse.mybir as mybir
from concourse.bass_utils import run_bass_kernel_spmd

F32 = mybir.dt.float32
BF16 = mybir.dt.bfloat16
AF = mybir.ActivationFunctionType
ALU = mybir.AluOpType

D = 2048
KC = 16
IN_W = 12352
FFN = 5632
EPS = 1e-5
ALPHA = 8.0 ** 0.25
C_ZQ, C_ZKV, C_ZKR, C_SGU, C_SGV, C_DQ, C_DK, C_DV, C_GATE = 0, 512, 1024, 1088, 2112, 3136, 4160, 5184, 6208
PAIRS = [[0, 1], [2, 3], [4, 5], [6, 7]]
QUADS = [[0, 1, 2, 3], [4, 5, 6, 7]]
WSPEC = {
    "ada": ("ada_w", 2048, 12288, 32),
    "in": ("w_in", 2048, 12352, 32),
    "uq": ("mla_w_uq", 512, 1536, 128),
    "ukv": ("mla_w_ukv", 512, 2048, 128),
    "br": ("w_branch", 3072, 2048, 256),
    "out": ("w_out", 2048, 2048, 256),
    "gu": ("ffn_w_gu", 2048, 11264, 32),
    "dn": ("ffn_w_down", 5632, 2048, 176),
}

COMPUTE = ("pe", "act", "dve", "pool")
DMA_POOLS = {"sp": 24, "pool": 16, "act": 4, "cc": 6}


class Op:
    __slots__ = ("eng", "fn", "kind", "deps", "signaled", "sem", "val", "prewait")


class Prog:
    def __init__(self, nc):
        self.nc = nc
        self.ops = []
        self.last_w = {}
        self.readers = {}
        self.regions = {}
        self.overl = {}

    def region(self, key, buf, lo, hi):
        ov = []
        for k2, (b2, lo2, hi2) in self.regions.items():
            if b2 == buf and lo < hi2 and lo2 < hi:
                ov.append(k2)
                self.overl[k2].append(key)
        self.regions[key] = (buf, lo, hi)
        self.overl[key] = ov

    def add(self, eng, fn, r=(), w=(), kind="c"):
        op = Op()
        op.eng, op.fn, op.kind = eng, fn, kind
        op.deps = set()
        op.signaled = False
        op.sem = None
        op.val = 0
        op.prewait = None
        wx = []
        for k in w:
            wx.append(k)
            for k2 in self.overl.get(k, ()):
                wx.append(k2)
        for k in r:
            p = self.last_w.get(k)
            if p is not None:
                op.deps.add(p)
        for k in wx:
            p = self.last_w.get(k)
            if p is not None:
                op.deps.add(p)
            rd = self.readers.get(k)
            if rd is not None:
                for q in rd[0].values():
                    op.deps.add(q)
                for q in rd[1]:
                    op.deps.add(q)
        for k in wx:
            self.last_w[k] = op
            self.readers[k] = ({}, [])
        for k in r:
            rd = self.readers.setdefault(k, ({}, []))
            if kind == "c":
                rd[0][eng] = op
            else:
                rd[1].append(op)
        op.deps.discard(op)
        if kind == "c" and eng == "pe":
            op.deps = {p for p in op.deps if not (p.kind == "c" and p.eng == "pe")}
        for p in op.deps:
            p.signaled = True
        self.ops.append(op)
        return op

    def emit(self, sems):
        nc = self.nc
        cnt = {e: 0 for e in COMPUTE}
        dcnt = {q: 0 for q in DMA_POOLS}
        for op in self.ops:
            if op.kind == "c":
                if op.signaled:
                    cnt[op.eng] += 1
                    op.sem = sems[op.eng]
                    op.val = cnt[op.eng]
            else:
                q = "cc" if op.kind == "cc" else op.eng
                n = DMA_POOLS[q]
                i = dcnt[q]
                dcnt[q] += 1
                inc = 1 if op.kind == "cc" else 16
                op.sem = sems["%s%d" % (q, i % n)]
                op.val = inc * (i // n + 1)
                if i >= n:
                    op.prewait = (op.sem, inc * (i // n))
        per_eng = {e: [] for e in ("pe", "act", "dve", "pool", "sp")}
        for op in self.ops:
            per_eng[op.eng].append(op)
        self.stats = {e: len(v) for e, v in per_eng.items()}

        def run(eng_name, e):
            waited = {}
            for op in per_eng[eng_name]:
                need = {}
                if op.prewait is not None:
                    need[id(op.prewait[0])] = op.prewait
                for p in op.deps:
                    k = id(p.sem)
                    if k not in need or need[k][1] < p.val:
                        need[k] = (p.sem, p.val)
                for k, (sem, val) in need.items():
                    if waited.get(k, 0) < val:
                        e.wait_ge(sem, val)
                        waited[k] = val
                ins = op.fn(e)
                if op.kind == "d":
                    ins.then_inc(op.sem, 16)
                elif op.kind == "cc":
                    ins.then_inc(op.sem)
                elif op.signaled:
                    ins.then_inc(op.sem, 1)
            last = {}
            for op in per_eng[eng_name]:
                if op.kind != "c":
                    last[id(op.sem)] = (op.sem, op.val)
            for k, (sem, val) in last.items():
                if waited.get(k, 0) < val:
                    e.wait_ge(sem, val)

        with nc.Block() as block:
            @block.tensor
            def _(e):
                run("pe", e)

            @block.scalar
            def _(e):
                run("act", e)

            @block.vector
            def _(e):
                run("dve", e)

            @block.gpsimd
            def _(e):
                run("pool", e)

            @block.sync
            def _(e):
                run("sp", e)


def sem_names():
    names = list(COMPUTE)
    for q, n in DMA_POOLS.items():
        names += ["%s%d" % (q, i) for i in range(n)]
    return names


def build(NLB, L, debug=()):
    NTOK = 128 + NLB * 512
    blocks = [(0, 128)] + [(128 + i * 512, 512) for i in range(NLB)]
    NB = len(blocks)
    NKT = 2 + NLB * 8
    NKEY = NKT * 128
    nc = bass.Bass("TRN2", target_bir_lowering=False)
    P = Prog(nc)
    es = ExitStack()

    def din(name, shape, dt=F32):
        return nc.dram_tensor(name, list(shape), dt, kind="ExternalInput").ap()

    def dscr(name, shape, dt=BF16):
        return nc.dram_tensor(name, list(shape), dt)

    xin = din("xin", [NTOK, D])
    cvec = din("cvec", [2, D])
    ropeM = din("ropeM", [2, 64, NTOK])
    ropeD = din("ropeD", [2, 128, NTOK])
    cmat = din("cmat", [4, 128, 128])
    ada_b = din("ada_b", [L, 6 * D])
    q_norm = din("mla_q_norm", [L, 512])
    kv_norm = din("mla_kv_norm", [L, 512])
    sgu_ln_g = din("sgu_ln_g", [L, 1024])
    sgu_ln_b = din("sgu_ln_b", [L, 1024])
    sgu_w = din("sgu_w", [L, 8, 128, 128])
    sgu_b = din("sgu_b", [L, 1024])
    diff_lam = din("diff_lam", [L, 4, 128])
    diff_subln = din("diff_subln", [L, 256])
    ln1_g = din("ln1_g", [L, D])
    ln1_b = din("ln1_b", [L, D])
    ln2_g = din("ln2_g", [L, D])
    ln2_b = din("ln2_b", [L, D])
    out = nc.dram_tensor("out", [NLB * 512, D], F32, kind="ExternalOutput").ap()

    wsh, wsb, wbt = {}, {}, {}
    for wn, (iname, K_, N_, R_) in WSPEC.items():
        wsh[wn] = din(iname, [L, K_ // 4, N_])
        wsb[wn] = [dscr("ws_%s%d" % (wn, l), [K_ // 4, N_]) for l in range(L)]
        wbt[wn] = [dscr("wb_%s%d" % (wn, l), [K_, N_]) for l in range(L)]
    wb_in = [t.ap() for t in wbt["in"]]
    wb_uq = [t.ap() for t in wbt["uq"]]
    wb_ukv = [t.ap() for t in wbt["ukv"]]
    wb_br = [t.ap() for t in wbt["br"]]
    wb_out = [t.ap() for t in wbt["out"]]
    wb_gu = [t.ap() for t in wbt["gu"]]
    wb_dn = [t.ap() for t in wbt["dn"]]
    wb_ada = [t.ap() for t in wbt["ada"]]

    def WK(wn, l):
        K_, R_ = WSPEC[wn][1], WSPEC[wn][3]
        return [("wb", wn, l, c) for c in range(K_ // (4 * R_))]
    hT = dscr("hT", [D, NTOK], F32).ap()
    xmT = dscr("xmT", [D, NTOK]).ap()
    qT = dscr("qT", [1536, NTOK]).ap()
    qdT = dscr("qdT", [1024, NTOK]).ap()
    gT = dscr("gT", [3 * D, NTOK]).ap()
    ysT = dscr("ysT", [1024, NTOK]).ap()
    yaT = dscr("yaT", [1024, NTOK]).ap()
    ydT = dscr("ydT", [1024, NTOK]).ap()
    exf_loc = [dscr("exf_loc%d" % b, [2112, T]) for b, (t0, T) in enumerate(blocks)]
    exf_a = [dscr("exf_a%d" % b, [2 * 1024, T]) for b, (t0, T) in enumerate(blocks)]
    exf_b = [dscr("exf_b%d" % b, [2 * 1088, T]) for b, (t0, T) in enumerate(blocks)]
    ext_loc = [dscr("ext_loc%d" % b, [T, 2048]) for b, (t0, T) in enumerate(blocks)]
    ext_all = [dscr("ext_all%d" % b, [2 * T, 2048]) for b, (t0, T) in enumerate(blocks)]
    dbg = {}
    for name, shape, dt in debug:
        dbg[name] = nc.dram_tensor("dbg_" + name, list(shape), dt, kind="ExternalOutput").ap()

    NBF = 49152
    NFP = 16384
    arB = es.enter_context(nc.sbuf_tensor("arB", [128, NBF], BF16))
    arF = es.enter_context(nc.sbuf_tensor("arF", [128, NFP], F32))
    consts = {}

    def cst(name, shape, dt):
        consts[name] = es.enter_context(nc.sbuf_tensor(name, shape, dt))
        return consts[name]

    identf = cst("identf", [128, 128], F32)
    identb = cst("identb", [128, 128], BF16)
    r64b = cst("r64b", [128, 128], BF16)
    r128b = cst("r128b", [128, 128], BF16)
    onesf = cst("onesf", [128, 128], F32)
    onesb = cst("onesb", [128, 128], BF16)
    cstage = cst("cstage", [128, 128], F32)
    vecT = cst("vecT", [128, 80], F32)
    modT = cst("modT", [128, 2, 96], F32)
    der = cst("der", [128, 2, 2, 8, 16], F32)
    smalls = cst("smalls", [128, 16], F32)
    fus = cst("fus", [128, 2, 2, 16], F32)
    scT = cst("scT", [128, 16, 2], BF16)
    wsT = cst("wsT", [128, 8, 128], BF16)
    bsB = cst("bsB", [128, 1024], F32)
    sgBg = cst("sgBg", [128, 1024], F32)
    sgBb = cst("sgBb", [128, 1024], F32)
    psb = [es.enter_context(nc.psum_tensor("ps%d" % i, [128, 512], F32)) for i in range(8)]
    psbf = None
    sems = {n: es.enter_context(nc.semaphore(n)) for n in sem_names()}

    class Arena:
        def __init__(self, t, name, n):
            self.t, self.name, self.n, self.off = t, name, n, 0
            self.flat = {}

        def reset(self):
            self.off = 0

        def _reg(self, key, lo, n):
            if key not in P.regions:
                P.region(key, self.name, lo, lo + n)
            else:
                assert P.regions[key] == (self.name, lo, lo + n), (key, P.regions[key], lo, n)

        def view(self, key, *shape, at=None, sub=False):
            n = 1
            for s in shape:
                n *= s
            if at is None:
                lo = self.off
                self.off += n
            else:
                lo = at
            assert lo + n <= self.n, (key, lo, n, self.n)
            self._reg(key, lo, n)
            if sub:
                m = n // shape[0]
                for i in range(shape[0]):
                    self._reg((key, i), lo + i * m, m)
            v = self.t[:, lo:lo + n]
            self.flat[key] = v
            if len(shape) == 2:
                v = v.rearrange("p (a b) -> p a b", b=shape[1])
            elif len(shape) == 3:
                v = v.rearrange("p (a b c) -> p a b c", b=shape[1], c=shape[2])
            return v

    AB = Arena(arB, "arB", NBF)
    AFp = Arena(arF, "arF", NFP)

    psctr = [0]

    def ps_next(pool=8):
        i = psctr[0] % pool
        psctr[0] += 1
        return i

    def PSK(i):
        return "ps%d" % i

    rot = {}

    def rotv(name, n):
        i = rot.get(name, 0)
        rot[name] = i + 1
        return i % n

    def dma(q, out_ap, in_ap, r, w):
        P.add(q, lambda e: e.dma_start(out=out_ap, in_=in_ap), r=r, w=w, kind="d")

    def mm(ps_i, rows, cols, lhsT, rhs, start, stop, r):
        P.add("pe", lambda e: e.matmul(psb[ps_i][:rows, :cols], lhsT=lhsT, rhs=rhs, start=start, stop=stop),
              r=r, w=[PSK(ps_i)])

    def act(out_ap, in_ap, func, r, w, scale=None, bias=None):
        kw = {}
        if scale is not None:
            kw["scale"] = scale
        if bias is not None:
            kw["bias"] = bias
        P.add("act", lambda e: e.activation(out=out_ap, in_=in_ap, func=func, **kw), r=r, w=w)

    def tt(eng, out_ap, a, b, op, r, w):
        P.add(eng, lambda e: e.tensor_tensor(out=out_ap, in0=a, in1=b, op=op), r=r, w=w)

    def ts(eng, out_ap, a, s1, s2, op0, op1, r, w):
        P.add(eng, lambda e: e.tensor_scalar(out=out_ap, in0=a, scalar1=s1, scalar2=s2, op0=op0, op1=op1), r=r, w=w)

    def stt(eng, out_ap, a, scalar, b, op0, op1, r, w):
        P.add(eng, lambda e: e.scalar_tensor_tensor(out=out_ap, in0=a, scalar=scalar, in1=b, op0=op0, op1=op1), r=r, w=w)

    def rsqrt_chain(dst, dst_key, src, src_key, mult, add):
        ts("dve", dst, src, mult, add, ALU.mult, ALU.add, [src_key], [dst_key])
        act(dst, dst, AF.Sqrt, [dst_key], [dst_key])
        P.add("dve", lambda e: e.reciprocal(out=dst, in_=dst), r=[dst_key], w=[dst_key])

    dma("sp", identf[:], cmat[0], [], ["identf"])
    P.add("dve", lambda e: e.tensor_copy(out=identb[:], in_=identf[:]), r=["identf"], w=["identb"])
    dma("sp", cstage[:], cmat[1], [], ["cstage"])
    P.add("dve", lambda e: e.tensor_copy(out=r64b[:], in_=cstage[:]), r=["cstage"], w=["r64b"])
    dma("sp", cstage[:], cmat[2], ["cstage"], ["cstage"])
    P.add("dve", lambda e: e.tensor_copy(out=r128b[:], in_=cstage[:]), r=["cstage"], w=["r128b"])
    dma("sp", onesf[:], cmat[3], [], ["onesf"])
    P.add("dve", lambda e: e.tensor_copy(out=onesb[:], in_=onesf[:]), r=["onesf"], w=["onesb"])

    dma("sp", cstage[0:16, :], cvec[0].rearrange("(c p) -> c p", p=128), ["cstage"], ["cstage"])
    dma("sp", cstage[16:32, :], cvec[1].rearrange("(c p) -> c p", p=128), ["cstage"], ["cstage"])
    pi = ps_next()
    P.add("pe", lambda e: e.transpose(out=psb[pi][:, :32], in_=cstage[0:32, :], identity=identf[0:32, 0:32]),
          r=["cstage", "identf"], w=[PSK(pi)])
    for r_ in range(2):
        act(scT[:, :, r_], psb[pi][:, r_ * 16:(r_ + 1) * 16], AF.Silu, [PSK(pi)], ["scT"])

    def gather_layer(l, names):
        for wn in names:
            iname, K_, N_, R_ = WSPEC[wn]
            skeys = []
            for c0 in range(0, N_, 2048):
                c1 = min(N_, c0 + 2048)
                sk = ("ws", wn, l, c0)
                skeys.append(sk)
                dma("pool", wsb[wn][l].ap()[:, c0:c1], wsh[wn][l][:, c0:c1], [], [sk])
            for c in range(K_ // (4 * R_)):
                sa = wsb[wn][l].ap()[c * R_:(c + 1) * R_, :]
                da = wbt[wn][l].ap()[c * 4 * R_:(c + 1) * 4 * R_, :]
                P.add("pool", lambda e, sa=sa, da=da: e.collective_compute(
                    "AllGather", ALU.bypass, replica_groups=QUADS, ins=[sa.opt()], outs=[da.opt()]),
                    r=skeys, w=[("wb", wn, l, c)], kind="cc")

    def wtile():
        i = rotv("wt", 2)
        return AB_w[i], "wt%d" % i

    def wview(wk, KCn, gc):
        return AB.flat[wk][:, 0:KCn * gc].rearrange("p (a b) -> p a b", b=gc)

    VQ, VKV, VL1G, VL1B, VL2G, VL2B, VSUB, VLAM = 0, 4, 8, 24, 40, 56, 72, 74
    DA1, DB1, DG1, DA2, DB2, DG2 = 0, 1, 2, 3, 4, 5
    DLG, DLB = 6, 7

    def layer_vectors(l):
        lam_init = 0.8 - 0.6 * math.exp(-0.3 * l)
        rows = [(q_norm[l], 4, VQ), (kv_norm[l], 4, VKV), (ln1_g[l], 16, VL1G), (ln1_b[l], 16, VL1B),
                (ln2_g[l], 16, VL2G), (ln2_b[l], 16, VL2B), (diff_subln[l], 2, VSUB)]
        for src, n, r0 in rows:
            dma("sp", cstage[r0:r0 + n, :], src.rearrange("(c p) -> c p", p=128), ["cstage"], ["cstage"])
        dma("sp", cstage[VLAM:VLAM + 4, :], diff_lam[l], ["cstage"], ["cstage"])
        pi = ps_next()
        P.add("pe", lambda e: e.transpose(out=psb[pi][:, :78], in_=cstage[0:78, :], identity=identf[0:78, 0:78]),
              r=["cstage", "identf"], w=[PSK(pi)])
        P.add("dve", lambda e: e.tensor_copy(out=vecT[:, 0:78], in_=psb[pi][:, :78]), r=[PSK(pi)], w=["vecT"])
        tt("dve", smalls[:, 0:1], vecT[:, VLAM:VLAM + 1], vecT[:, VLAM + 1:VLAM + 2], ALU.mult, ["vecT"], ["smalls"])
        tt("dve", smalls[:, 1:2], vecT[:, VLAM + 2:VLAM + 3], vecT[:, VLAM + 3:VLAM + 4], ALU.mult, ["vecT"], ["smalls"])
        pj = ps_next()
        mm(pj, 128, 2, onesf[:], smalls[:, 0:2], True, True, ["onesf", "smalls"])
        act(smalls[:, 2:4], psb[pj][:, 0:2], AF.Exp, [PSK(pj)], ["smalls"])
        tt("dve", smalls[:, 4:5], smalls[:, 3:4], smalls[:, 2:3], ALU.subtract, ["smalls"], ["smalls"])
        ts("dve", smalls[:, 5:6], smalls[:, 4:5], 1.0, -lam_init, ALU.mult, ALU.add, ["smalls"], ["smalls"])
        ts("dve", smalls[:, 6:8], vecT[:, VSUB:VSUB + 2], (1.0 - lam_init), 0.0, ALU.mult, ALU.add, ["vecT"], ["smalls"])
        dma("sp", bsB[:], sgu_b[l].partition_broadcast(128), [], ["bsB"])
        dma("sp", sgBg[:], sgu_ln_g[l].partition_broadcast(128), [], ["sgBg"])
        dma("sp", sgBb[:], sgu_ln_b[l].partition_broadcast(128), [], ["sgBb"])
        for g in range(8):
            dma("sp", cstage[:], sgu_w[l, g], ["cstage"], ["cstage"])
            pk = ps_next()
            P.add("pe", lambda e, pk=pk: e.transpose(out=psb[pk][:, :128], in_=cstage[:], identity=identf[:]),
                  r=["cstage", "identf"], w=[PSK(pk)])
            P.add("dve", lambda e, pk=pk, g=g: e.tensor_copy(out=wsT[:, g, :], in_=psb[pk][:, :128]), r=[PSK(pk)], w=["wsT"])

    def layer_mod(l):
        for c0 in range(0, 96, 32):
            dma("sp", cstage[0:32, :], ada_b[l, c0 * 128:(c0 + 32) * 128].rearrange("(c p) -> c p", p=128),
                ["cstage"], ["cstage"])
            pi = ps_next()
            P.add("pe", lambda e, pi=pi: e.transpose(out=psb[pi][:, :32], in_=cstage[0:32, :], identity=identf[0:32, 0:32]),
                  r=["cstage", "identf"], w=[PSK(pi)])
            for r_ in range(2):
                P.add("dve", lambda e, pi=pi, r_=r_, c0=c0: e.tensor_copy(out=modT[:, r_, c0:c0 + 32], in_=psb[pi][:, :32]),
                      r=[PSK(pi)], w=["modT"])
        for g in range(24):
            wt, wk = wtile()
            dma("sp", wt[:, :, :], wb_ada[l][:, g * 512:(g + 1) * 512].rearrange("(kc p) c -> p kc c", p=128), WK("ada", l), [wk])
            pi = ps_next()
            for c in range(4):
                for kc in range(16):
                    P.add("pe", lambda e, pi=pi, c=c, kc=kc, wt=wt: e.matmul(
                        psb[pi][:, 2 * c:2 * c + 2], lhsT=wt[:, kc, c * 128:(c + 1) * 128], rhs=scT[:, kc, :],
                        start=(kc == 0), stop=(kc == 15)), r=[wk, "scT"], w=[PSK(pi)])
            for r_ in range(2):
                ch = g * 4
                P.add("dve", lambda e, pi=pi, r_=r_, ch=ch: e.tensor_tensor(
                    out=modT[:, r_, ch:ch + 4], in0=psb[pi][:, r_:8:2], in1=modT[:, r_, ch:ch + 4], op=ALU.add),
                    r=[PSK(pi), "modT"], w=["modT"])
        lp = l % 2
        dk = "der%d" % lp
        for r_ in range(2):
            ts("dve", der[:, r_, lp, DA1, :], modT[:, r_, 16:32], 1.0, 1.0, ALU.mult, ALU.add, ["modT"], [dk])
            P.add("dve", lambda e, r_=r_: e.tensor_copy(out=der[:, r_, lp, DB1, :], in_=modT[:, r_, 0:16]), r=["modT"], w=[dk])
            ts("dve", der[:, r_, lp, DG1, :], modT[:, r_, 32:48], 1.0 / ALPHA, 0.0, ALU.mult, ALU.add, ["modT"], [dk])
            ts("dve", der[:, r_, lp, DA2, :], modT[:, r_, 64:80], 1.0, 1.0, ALU.mult, ALU.add, ["modT"], [dk])
            P.add("dve", lambda e, r_=r_: e.tensor_copy(out=der[:, r_, lp, DB2, :], in_=modT[:, r_, 48:64]), r=["modT"], w=[dk])
            ts("dve", der[:, r_, lp, DG2, :], modT[:, r_, 80:96], 1.0 / ALPHA, 0.0, ALU.mult, ALU.add, ["modT"], [dk])

    def fused_affine(r_, gcol, bcol, lp, acol, bbcol):
        dk = "der%d" % lp
        tt("dve", fus[:, r_, 0, :], vecT[:, gcol:gcol + 16], der[:, r_, lp, acol, :], ALU.mult, ["vecT", dk], ["fus"])
        tt("dve", fus[:, r_, 1, :], vecT[:, bcol:bcol + 16], der[:, r_, lp, acol, :], ALU.mult, ["vecT", dk], ["fus"])
        tt("dve", fus[:, r_, 1, :], fus[:, r_, 1, :], der[:, r_, lp, bbcol, :], ALU.add, ["fus", dk], ["fus"])

    def linear_fm(xt, xkey, KCn, wsrc, wkey, chunks, T, evac, pool=8):
        gw = 8192 // KCn
        i = 0
        while i < len(chunks):
            g0 = chunks[i][0]
            j = i
            while j < len(chunks) and chunks[j][0] + chunks[j][1] - g0 <= gw:
                j += 1
            gc = chunks[j - 1][0] + chunks[j - 1][1] - g0
            wt, wk = wtile()
            wv = wview(wk, KCn, gc)
            dma("sp", wv, wsrc[:, g0:g0 + gc].rearrange("(kc p) c -> p kc c", p=128), wkey, [wk])
            for (c, m) in chunks[i:j]:
                pi = ps_next(pool)
                for kc in range(KCn):
                    mm(pi, m, T, wv[:, kc, c - g0:c - g0 + m], xt[:, kc, :T], kc == 0, kc == KCn - 1, [wk, xkey])
                evac(c, m, pi)
            i = j

    def linear_tm(xt, xkey, KCn, wsrc, wkey, col0, ncols, T, evac, colsel=None):
        gw = 8192 // KCn
        for g0 in range(col0, col0 + ncols, gw):
            gc = min(gw, col0 + ncols - g0)
            wt, wk = wtile()
            wv = wview(wk, KCn, gc)
            dma("sp", wv, wsrc[:, g0:g0 + gc].rearrange("(kc p) c -> p kc c", p=128), wkey, [wk])
            evac(g0 - col0, gc, wv, wk)

    def rope(src, skey, Dm, T, cosv, sinv, tkey, dst, dkey, t1, t1k, t2, t2k):
        rm = r64b if Dm == 64 else r128b
        rk = "r64b" if Dm == 64 else "r128b"
        pi = ps_next()
        mm(pi, Dm, T, rm[:Dm, :Dm], src, True, True, [rk, skey])
        tt("pool", t1[:Dm, :T], src, cosv[:Dm, :T], ALU.mult, [skey, tkey], [t1k])
        tt("dve", t2[:Dm, :T], psb[pi][:Dm, :T], sinv[:Dm, :T], ALU.mult, [PSK(pi), tkey], [t2k])
        tt("dve", dst, t1[:Dm, :T], t2[:Dm, :T], ALU.add, [t1k, t2k], [dkey])

    def gelu_evac(ps_ap, pkey, T, dst, dkey, g1, g1k, g2, g2k):
        act(g1[:, :T], ps_ap, AF.Square, [pkey], [g1k])
        ts("pool", g1[:, :T], g1[:, :T], 0.044715, 1.0, ALU.mult, ALU.add, [g1k], [g1k])
        tt("dve", g2[:, :T], g1[:, :T], ps_ap, ALU.mult, [g1k, pkey], [g2k])
        act(g2[:, :T], g2[:, :T], AF.Sigmoid, [g2k], [g2k], scale=1.5957691216057308)
        tt("dve", dst, g2[:, :T], ps_ap, ALU.mult, [g2k, pkey], [dkey])

    AB.reset()
    AB_w = [AB.view("wt0", 16, 512), AB.view("wt1", 16, 512)]
    W_END_B = AB.off
    xm = AB.view("xm", 16, 512)
    X_END_B = AB.off

    def phase_A(l, last):
        AB.off = X_END_B
        AFp.reset()
        zg = [AB.view("zqg", 4, 512), AB.view("zkvg", 4, 512)]
        zsq = [AB.view("zqsq", 4, 512), AB.view("zkvsq", 4, 512)]
        zn = zg
        ZG, ZSQ = ["zqg", "zkvg"], ["zqsq", "zkvsq"]
        uT = AB.view("uT", 8, 512)
        vn = AB.view("vn", 1024)
        ysb = AB.view("ysb", 8, 512)
        st = [AB.view("st%d" % i, 512) for i in range(4)]
        so = [AB.view("so%d" % i, 512) for i in range(3)]
        vst = [AB.view("vst%d" % i, 1024) for i in range(2)]
        vt = AFp.view("vt", 4, 1024, sub=True)
        cosM, sinM = AFp.view("cosM", 512), AFp.view("sinM", 512)
        cosD, sinD = AFp.view("cosD", 512), AFp.view("sinD", 512)
        rr = [AFp.view("rq", 512), AFp.view("rkv", 512)]
        t1 = [AFp.view("t1_%d" % i, 512) for i in range(2)]
        t2 = [AFp.view("t2_%d" % i, 512) for i in range(2)]
        gt = [AFp.view("gt%d" % i, 512) for i in range(4)]
        bn = AFp.view("bn", 16)
        for b, (t0, T) in enumerate(blocks):
            r_ = 0 if b > 0 else 1
            ntt = T // 128
            xk = "xm"
            dma("sp", xm[:, :, :T], xmT[:, t0:t0 + T].rearrange("(kc p) t -> p kc t", p=128), [("xmT", b)], [xk])
            dma("sp", cosM[:64, :T], ropeM[0, :, t0:t0 + T], [], ["cosM"])
            dma("sp", sinM[:64, :T], ropeM[1, :, t0:t0 + T], [], ["sinM"])
            dma("sp", cosD[:, :T], ropeD[0, :, t0:t0 + T], [], ["cosD"])
            dma("sp", sinD[:, :T], ropeD[1, :, t0:t0 + T], [], ["sinD"])
            wsrc, wkey = wb_in[l], WK("in", l)
            exf, exk = exf_loc[b].ap(), ("exf_loc", b)
            ext, etk = ext_loc[b].ap(), ("ext_loc", b)

            for zi, (c0, nv) in enumerate(((C_ZQ, VQ), (C_ZKV, VKV))):
                def ev(c, m, pi, zi=zi, c0=c0, nv=nv):
                    ci = (c - c0) // 128
                    act(zsq[zi][:, ci, :T], psb[pi][:, :T], AF.Square, [PSK(pi)], [ZSQ[zi]])
                    act(zg[zi][:, ci, :T], psb[pi][:, :T], AF.Identity, [PSK(pi), "vecT"], [ZG[zi]],
                        scale=vecT[:, nv + ci:nv + ci + 1])
                linear_fm(xm, xk, 16, wsrc, wkey, [(c0 + i * 128, 128) for i in range(4)], T, ev)
                pj = ps_next()
                for ci in range(4):
                    mm(pj, 128, T, onesb[:], zsq[zi][:, ci, :T], ci == 0, ci == 3, ["onesb", ZSQ[zi]])
                rk = "rq" if zi == 0 else "rkv"
                rsqrt_chain(rr[zi][:, :T], rk, psb[pj][:, :T], PSK(pj), 1.0 / 512.0, EPS)
                for ci in range(4):
                    tt("dve" if ci % 2 else "pool", zn[zi][:, ci, :T], zg[zi][:, ci, :T], rr[zi][:, :T], ALU.mult,
                       [ZG[zi], rk], [ZG[zi]])

            def ev_kr(c, m, pi):
                i = rotv("st", 4)
                act(st[i][:64, :T], psb[pi][:64, :T], AF.Copy, [PSK(pi)], ["st%d" % i])
                j, k = rotv("so", 3), rotv("t", 2)
                rope(st[i][:64, :T], "st%d" % i, 64, T, cosM, sinM, "cosM", so[j][:64, :T], "so%d" % j,
                     t1[k], "t1_%d" % k, t2[k], "t2_%d" % k)
                dma("pool", exf[2048:2112, :], so[j][:64, :T], ["so%d" % j], [exk])
            linear_fm(xm, xk, 16, wsrc, wkey, [(C_ZKR, 64)], T, ev_kr)

            if not (last and b == 0):
                def ev_q(c, m, pi):
                    i = rotv("st", 4)
                    if m == 128:
                        h = c // 192
                        act(st[i][:, :T], psb[pi][:, :T], AF.Copy, [PSK(pi)], ["st%d" % i])
                        dma("pool", qT[h * 192:h * 192 + 128, t0:t0 + T], st[i][:, :T], ["st%d" % i], [("qT", b)])
                    else:
                        h = c // 192
                        act(st[i][:64, :T], psb[pi][:64, :T], AF.Copy, [PSK(pi)], ["st%d" % i])
                        j, k = rotv("so", 3), rotv("t", 2)
                        rope(st[i][:64, :T], "st%d" % i, 64, T, cosM, sinM, "cosM", so[j][:64, :T], "so%d" % j,
                             t1[k], "t1_%d" % k, t2[k], "t2_%d" % k)
                        dma("pool", qT[h * 192 + 128:h * 192 + 192, t0:t0 + T], so[j][:64, :T], ["so%d" % j], [("qT", b)])
                chq = []
                for h in range(8):
                    chq += [(h * 192, 128), (h * 192 + 128, 64)]
                linear_fm(zn[0], ZG[0], 4, wb_uq[l], WK("uq", l), chq, T, ev_q)

            def ev_k(c, m, pi):
                h = c // 256
                i = rotv("st", 4)
                act(st[i][:, :T], psb[pi][:, :T], AF.Copy, [PSK(pi)], ["st%d" % i])
                dma("pool", exf[h * 128:(h + 1) * 128, :], st[i][:, :T], ["st%d" % i], [exk])
            wt, wk = wtile()
            wv = wview(wk, 4, 2048)
            dma("sp", wv, wb_ukv[l].rearrange("(kc p) c -> p kc c", p=128), WK("ukv", l), [wk])
            for h in range(8):
                pi = ps_next()
                for kc in range(4):
                    mm(pi, 128, T, wv[:, kc, h * 256:h * 256 + 128], zn[1][:, kc, :T], kc == 0, kc == 3, [wk, ZG[1]])
                ev_k(h * 256, 128, pi)
            wv4 = wv.rearrange("p a (h c) -> p a h c", c=256)
            for tti in range(ntt):
                vi = rotv("vst", 2)
                for hg in range(2):
                    pi = ps_next()
                    for kc in range(4):
                        mm(pi, 128, 512, zn[1][:, kc, tti * 128:(tti + 1) * 128], wv4[:, kc, hg * 4:hg * 4 + 4, 128:256],
                           kc == 0, kc == 3, [wk, ZG[1]])
                    act(vst[vi][:, hg * 512:(hg + 1) * 512], psb[pi][:, :512], AF.Copy, [PSK(pi)], ["vst%d" % vi])
                dma("pool", ext[tti * 128:(tti + 1) * 128, 0:1024], vst[vi][:, :], ["vst%d" % vi], [etk])

            for (c0, isq) in ((C_DQ, True), (C_DK, False)):
                if isq and last and b == 0:
                    continue

                def ev_d(c, m, pi, c0=c0, isq=isq):
                    jh = (c - c0) // 128
                    i = rotv("st", 4)
                    act(st[i][:, :T], psb[pi][:, :T], AF.Copy, [PSK(pi)], ["st%d" % i])
                    j, k = rotv("so", 3), rotv("t", 2)
                    rope(st[i][:, :T], "st%d" % i, 128, T, cosD, sinD, "cosD", so[j][:, :T], "so%d" % j,
                         t1[k], "t1_%d" % k, t2[k], "t2_%d" % k)
                    if isq:
                        dma("pool", qdT[jh * 128:(jh + 1) * 128, t0:t0 + T], so[j][:, :T], ["so%d" % j], [("qdT", b)])
                    else:
                        dma("pool", exf[1024 + jh * 128:1024 + (jh + 1) * 128, :], so[j][:, :T], ["so%d" % j], [exk])
                linear_fm(xm, xk, 16, wsrc, wkey, [(c0 + i * 128, 128) for i in range(8)], T, ev_d)

            def ev_dv(off, gc, wv, wk):
                for tti in range(ntt):
                    pi = ps_next()
                    for kc in range(16):
                        mm(pi, 128, gc, xm[:, kc, tti * 128:(tti + 1) * 128], wv[:, kc, :gc], kc == 0, kc == 15, [wk, xk])
                    vi = rotv("vst", 2)
                    act(vst[vi][:, :gc], psb[pi][:, :gc], AF.Copy, [PSK(pi)], ["vst%d" % vi])
                    dma("pool", ext[tti * 128:(tti + 1) * 128, 1024 + off:1024 + off + gc], vst[vi][:, :gc],
                        ["vst%d" % vi], [etk])
            linear_tm(xm, xk, 16, wsrc, wkey, C_DV, 1024, T, ev_dv)

            def cc(src_t, r0, r1, dst_t, rkey, wkey_):
                sa = src_t.ap()[r0:r1, :]
                P.add("pool", lambda e: e.collective_compute("AllGather", ALU.bypass, replica_groups=PAIRS,
                                                             ins=[sa.opt()], outs=[dst_t.ap().opt()]),
                      r=[rkey], w=[wkey_], kind="cc")
            cc(exf_loc[b], 0, 1024, exf_a[b], exk, ("exf_a", b))
            cc(exf_loc[b], 1024, 2112, exf_b[b], exk, ("exf_b", b))
            cc(ext_loc[b], 0, T, ext_all[b], etk, ("ext_all", b))

            if last and b == 0:
                continue
            def ev_g(c, m, pi):
                i = rotv("st", 4)
                act(st[i][:, :T], psb[pi][:, :T], AF.Sigmoid, [PSK(pi)], ["st%d" % i])
                dma("pool", gT[c - C_GATE:c - C_GATE + 128, t0:t0 + T], st[i][:, :T], ["st%d" % i], [("gT", b)])
            linear_fm(xm, xk, 16, wsrc, wkey, [(C_GATE + i * 128, 128) for i in range(48)], T, ev_g)

            def ev_u(c, m, pi):
                ci = (c - C_SGU) // 128
                a, bb = rotv("gt", 2), rotv("gt2", 2)
                gelu_evac(psb[pi][:, :T], PSK(pi), T, uT[:, ci, :T], "uT", gt[a], "gt%d" % a, gt[2 + bb], "gt%d" % (2 + bb))
            linear_fm(xm, xk, 16, wsrc, wkey, [(C_SGU + i * 128, 128) for i in range(8)], T, ev_u)

            def ev_v(off, gc, wv, wk):
                for tti in range(ntt):
                    pi = ps_next()
                    for kc in range(16):
                        mm(pi, 128, gc, xm[:, kc, tti * 128:(tti + 1) * 128], wv[:, kc, :gc], kc == 0, kc == 15, [wk, xk])
                    a, bb = rotv("gt", 2), rotv("gt2", 2)
                    gelu_evac(psb[pi][:, :gc], PSK(pi), gc, vt[:, tti, off:off + gc], ("vt", tti),
                              gt[a], "gt%d" % a, gt[2 + bb], "gt%d" % (2 + bb))
            linear_tm(xm, xk, 16, wsrc, wkey, C_SGV, 1024, T, ev_v)

            for tti in range(ntt):
                vk = ("vt", tti)
                for hh in range(2):
                    P.add("dve", lambda e, tti=tti, hh=hh: e.bn_stats(out=bn[:, hh * 6:(hh + 1) * 6],
                                                                      in_=vt[:, tti, hh * 512:(hh + 1) * 512]), r=[vk], w=["bn"])
                P.add("dve", lambda e: e.bn_aggr(out=bn[:, 12:14], in_=bn[:, 0:12].rearrange("p (a b) -> p a b", b=6)),
                      r=["bn"], w=["bn"])
                rsqrt_chain(bn[:, 14:15], "bn", bn[:, 13:14], "bn", 1.0, EPS)
                stt("dve", bn[:, 15:16], bn[:, 12:13], -1.0, bn[:, 14:15], ALU.mult, ALU.mult, ["bn"], ["bn"])
                act(vt[:, tti, :], vt[:, tti, :], AF.Identity, [vk, "bn"], [vk], scale=bn[:, 14:15], bias=bn[:, 15:16])
                tt("pool", vt[:, tti, :], vt[:, tti, :], sgBg[:], ALU.mult, [vk, "sgBg"], [vk])
                tt("dve", vn[:, :], vt[:, tti, :], sgBb[:], ALU.add, [vk, "sgBb"], ["vn"])
                for hg in range(2):
                    pi = ps_next()
                    for g4 in range(4):
                        g = hg * 4 + g4
                        P.add("pe", lambda e, pi=pi, g=g, g4=g4: e.matmul(psb[pi][:, g4 * 128:(g4 + 1) * 128],
                                                                           lhsT=vn[:, g * 128:(g + 1) * 128], rhs=wsT[:, g, :],
                                                                           start=True, stop=True),
                              r=["vn", "wsT"], w=[PSK(pi)])
                    k = rotv("t", 2)
                    tt("dve", t2[k][:, :512], psb[pi][:, :512], bsB[:, hg * 512:(hg + 1) * 512], ALU.add,
                       [PSK(pi), "bsB"], ["t2_%d" % k])
                    tt("pool", ysb[:, hg * 4:hg * 4 + 4, tti * 128:(tti + 1) * 128],
                       t2[k][:, :512].rearrange("p (g c) -> p g c", c=128),
                       uT[:, hg * 4:hg * 4 + 4, tti * 128:(tti + 1) * 128], ALU.mult, ["t2_%d" % k, "uT"], ["ysb"])
            dma("pool", ysT[:, t0:t0 + T].rearrange("(g c) t -> c g t", c=128), ysb[:, :, :T], ["ysb"], [("ysT", b)])

    def phase_B(l, last):
        AB.off = 0
        AFp.reset()
        KT = [AB.view("KT%d" % i, NKEY) for i in range(2)]
        KR = AB.view("KR", NKEY)
        VV = [AB.view("VV%d" % i, NKT, 256) for i in range(2)]
        QN = [AB.view("QN%d" % i, 512) for i in range(2)]
        QR = [AB.view("QR%d" % i, 512) for i in range(2)]
        PT = [AB.view("PT%d" % i, 512) for i in range(3)]
        OB = [AB.view("OB%d" % i, 2, 512) for i in range(2)]
        SQ = AB.view("SQd", 2, 512)
        rl = AFp.view("rl", 512)
        om = AFp.view("om", 2, 2, 512)
        dd = AFp.view("dd", 2, 512)
        rd = AFp.view("rdd", 512)
        qblocks = [bb for bb in range(NB) if not (last and bb == 0)]

        def key_pieces():
            res = [(0, 0, 0, 128), (1, 0, 1, 128)]
            i = 2
            for bb in range(1, NB):
                for r_ in range(2):
                    res.append((i, bb, r_, 512))
                    i += 4
            return res
        pieces = key_pieces()
        allA = [("exf_a", bb) for bb in range(NB)]
        allB = [("exf_b", bb) for bb in range(NB)]
        allT = [("ext_all", bb) for bb in range(NB)]

        for (i0, bb, r_, T) in pieces:
            dma("sp", KR[:64, i0 * 128:i0 * 128 + T], exf_b[bb].ap()[r_ * 1088 + 1024:r_ * 1088 + 1088, :], allB, ["KR"])

        def load_head(kind, j):
            ki = rotv("KT", 2)
            for (i0, bb, r_, T) in pieces:
                if kind == "m":
                    src = exf_a[bb].ap()[r_ * 1024 + j * 128:r_ * 1024 + (j + 1) * 128, :]
                    dma("sp", KT[ki][:, i0 * 128:i0 * 128 + T], src, allA, ["KT%d" % ki])
                else:
                    src = exf_b[bb].ap()[r_ * 1088 + j * 128:r_ * 1088 + (j + 1) * 128, :]
                    dma("sp", KT[ki][:, i0 * 128:i0 * 128 + T], src, allB, ["KT%d" % ki])
            return ki

        def load_v(kind, h):
            vi = rotv("VV", 2)
            for (i0, bb, r_, T) in pieces:
                if kind == "m":
                    src = ext_all[bb].ap()[r_ * T:(r_ + 1) * T, h * 128:(h + 1) * 128].rearrange("(t p) c -> p t c", p=128)
                    dma("sp", VV[vi][:, i0:i0 + T // 128, 0:128], src, allT, ["VV%d" % vi])
                else:
                    src = ext_all[bb].ap()[r_ * T:(r_ + 1) * T, 1024 + h * 256:1024 + (h + 1) * 256].rearrange(
                        "(t p) c -> p t c", p=128)
                    dma("sp", VV[vi][:, i0:i0 + T // 128, 0:256], src, allT, ["VV%d" % vi])
            return vi

        def attend(kind, ki, vi, qn, qnk, qr, qrk, Tq, ktiles, scale, nv):
            n = len(ktiles)
            sbank = {}

            def issue_s(i):
                kt = ktiles[i]
                pi = 3 + ps_next(5)
                sbank[i] = pi
                if kind == "m":
                    mm(pi, 128, Tq, KT[ki][:, kt * 128:(kt + 1) * 128], qn, True, False, ["KT%d" % ki, qnk])
                    mm(pi, 128, Tq, KR[:64, kt * 128:(kt + 1) * 128], qr, False, True, ["KR", qrk])
                else:
                    mm(pi, 128, Tq, KT[ki][:, kt * 128:(kt + 1) * 128], qn, True, True, ["KT%d" % ki, qnk])
            for i in range(min(2, n)):
                issue_s(i)
            for i in range(n):
                kt = ktiles[i]
                pi = sbank[i]
                pt = rotv("PT", 3)
                act(PT[pt][:, :Tq], psb[pi][:, :Tq], AF.Exp, [PSK(pi)], ["PT%d" % pt], scale=scale)
                if i + 2 < n:
                    issue_s(i + 2)
                for c in range(nv):
                    mm(c, 128, Tq, VV[vi][:, kt, c * 128:(c + 1) * 128], PT[pt][:, :Tq], i == 0, i == n - 1,
                       ["VV%d" % vi, "PT%d" % pt])
                mm(2, 128, Tq, onesb[:], PT[pt][:, :Tq], i == 0, i == n - 1, ["onesb", "PT%d" % pt])

        for h in range(8):
            ki = load_head("m", h)
            vi = load_v("m", h)
            for bb in qblocks:
                t0, Tq = blocks[bb]
                qi = rotv("Q", 2)
                dma("sp", QN[qi][:, :Tq], qT[h * 192:h * 192 + 128, t0:t0 + Tq], [("qT", bb)], ["QN%d" % qi])
                dma("sp", QR[qi][:64, :Tq], qT[h * 192 + 128:h * 192 + 192, t0:t0 + Tq], [("qT", bb)], ["QR%d" % qi])
                ktiles = [0, 1] if bb == 0 else list(range(NKT))
                attend("m", ki, vi, QN[qi][:, :Tq], "QN%d" % qi, QR[qi][:64, :Tq], "QR%d" % qi, Tq, ktiles, 192.0 ** -0.5, 1)
                P.add("dve", lambda e, Tq=Tq: e.reciprocal(out=rl[:, :Tq], in_=psb[2][:, :Tq]), r=[PSK(2)], w=["rl"])
                oi = rotv("OB", 2)
                tt("dve", OB[oi][:, 0, :Tq], psb[0][:, :Tq], rl[:, :Tq], ALU.mult, [PSK(0), "rl"], ["OB%d" % oi])
                dma("pool", yaT[h * 128:(h + 1) * 128, t0:t0 + Tq], OB[oi][:, 0, :Tq], ["OB%d" % oi], [("yaT", bb)])

        for hd in range(4):
            vi = load_v("d", hd)
            kis = [load_head("d", hd * 2 + m_) for m_ in range(2)]
            for bb in qblocks:
                t0, Tq = blocks[bb]
                ktiles = [0, 1] if bb == 0 else list(range(NKT))
                for m_ in range(2):
                    qi = rotv("Q", 2)
                    j = hd * 2 + m_
                    dma("sp", QN[qi][:, :Tq], qdT[j * 128:(j + 1) * 128, t0:t0 + Tq], [("qdT", bb)], ["QN%d" % qi])
                    attend("d", kis[m_], vi, QN[qi][:, :Tq], "QN%d" % qi, None, None, Tq, ktiles, 128.0 ** -0.5, 2)
                    P.add("dve", lambda e, Tq=Tq: e.reciprocal(out=rl[:, :Tq], in_=psb[2][:, :Tq]), r=[PSK(2)], w=["rl"])
                    for c in range(2):
                        tt("dve", om[:, m_, c, :Tq], psb[c][:, :Tq], rl[:, :Tq], ALU.mult, [PSK(c), "rl"], ["om"])
                for c in range(2):
                    stt("dve", dd[:, c, :Tq], om[:, 1, c, :Tq], smalls[:, 5:6], om[:, 0, c, :Tq], ALU.mult, ALU.add,
                        ["om", "smalls"], ["dd"])
                    act(SQ[:, c, :Tq], dd[:, c, :Tq], AF.Square, ["dd"], ["SQd"])
                pi = 3 + ps_next(5)
                for c in range(2):
                    mm(pi, 128, Tq, onesb[:], SQ[:, c, :Tq], c == 0, c == 1, ["onesb", "SQd"])
                rsqrt_chain(rd[:, :Tq], "rdd", psb[pi][:, :Tq], PSK(pi), 1.0 / 256.0, EPS)
                oi = rotv("OB", 2)
                for c in range(2):
                    tt("dve", dd[:, c, :Tq], dd[:, c, :Tq], rd[:, :Tq], ALU.mult, ["dd", "rdd"], ["dd"])
                    act(OB[oi][:, c, :Tq], dd[:, c, :Tq], AF.Identity, ["dd", "smalls"], ["OB%d" % oi], scale=smalls[:, 6 + c:7 + c])
                dma("pool", ydT[hd * 256:(hd + 1) * 256, t0:t0 + Tq].rearrange("(c p) t -> p c t", p=128),
                    OB[oi][:, :, :Tq], ["OB%d" % oi], [("ydT", bb)])

    def ln_setup():
        v = {}
        v["yp"] = AFp.view("yp", 16, 512, sub=True)
        v["hc"] = [AFp.view("hc%d" % i, 512) for i in range(2)]
        v["sq"] = [AFp.view("lsq%d" % i, 512) for i in range(2)]
        v["mean"] = AFp.view("lmean", 512)
        v["rstd"] = AFp.view("lrstd", 512)
        v["nmr"] = AFp.view("lnmr", 512)
        v["ho"] = [AFp.view("lho%d" % i, 512) for i in range(2)]
        v["xo"] = [AB.view("lxo%d" % i, 512, at=NBF - 1024 + i * 512) for i in range(2)]
        return v

    def ln_chunk_in(v, dc, pi, T, b, t0, gsel, r_, lp):
        hi = rotv("hc", 2)
        dma("sp", v["hc"][hi][:, :T], hT[dc * 128:(dc + 1) * 128, t0:t0 + T], [("hT", b)], ["hc%d" % hi])
        stt("dve", v["yp"][:, dc, :T], psb[pi][:, :T], der[:, r_, lp, gsel, dc:dc + 1], v["hc"][hi][:, :T], ALU.mult, ALU.add,
            [PSK(pi), "der%d" % lp, "hc%d" % hi], [("yp", dc)])

    def ln_stats(v, dc, T):
        si = rotv("lsq", 2)
        act(v["sq"][si][:, :T], v["yp"][:, dc, :T], AF.Square, [("yp", dc)], ["lsq%d" % si])
        mm(6, 128, T, onesf[:], v["yp"][:, dc, :T], dc == 0, dc == 15, ["onesf", ("yp", dc)])
        mm(7, 128, T, onesf[:], v["sq"][si][:, :T], dc == 0, dc == 15, ["onesf", "lsq%d" % si])

    def ln_finish(v, T, b, t0, r_, l, which, final_out):
        mean, rstd, nmr = v["mean"], v["rstd"], v["nmr"]
        ts("dve", mean[:, :T], psb[6][:, :T], 1.0 / D, 0.0, ALU.mult, ALU.add, [PSK(6)], ["lmean"])
        tt("dve", nmr[:, :T], mean[:, :T], mean[:, :T], ALU.mult, ["lmean"], ["lnmr"])
        stt("dve", rstd[:, :T], psb[7][:, :T], 1.0 / D, nmr[:, :T], ALU.mult, ALU.subtract, [PSK(7), "lnmr"], ["lrstd"])
        rsqrt_chain(rstd[:, :T], "lrstd", rstd[:, :T], "lrstd", 1.0, EPS / (ALPHA * ALPHA))
        stt("dve", nmr[:, :T], mean[:, :T], -1.0, rstd[:, :T], ALU.mult, ALU.mult, ["lmean", "lrstd"], ["lnmr"])
        gcol, bcol = (VL1G, VL1B) if which == 1 else (VL2G, VL2B)
        for dc in range(16):
            yk = ("yp", dc)
            tt("dve", v["yp"][:, dc, :T], v["yp"][:, dc, :T], rstd[:, :T], ALU.mult, [yk, "lrstd"], [yk])
            tt("pool", v["yp"][:, dc, :T], v["yp"][:, dc, :T], nmr[:, :T], ALU.add, [yk, "lnmr"], [yk])
            hi = rotv("lho", 2)
            act(v["ho"][hi][:, :T], v["yp"][:, dc, :T], AF.Identity, [yk, "vecT"], ["lho%d" % hi],
                scale=vecT[:, gcol + dc:gcol + dc + 1], bias=vecT[:, bcol + dc:bcol + dc + 1])
            if final_out and b > 0:
                for tti in range(T // 128):
                    pi = ps_next(6)
                    P.add("pe", lambda e, pi=pi, hi=hi, tti=tti: e.transpose(out=psb[pi][:, :128],
                                                                             in_=v["ho"][hi][:, tti * 128:(tti + 1) * 128],
                                                                             identity=identf[:]),
                          r=["lho%d" % hi, "identf"], w=[PSK(pi)])
                    k = rotv("lsq", 2)
                    P.add("dve", lambda e, pi=pi, k=k: e.tensor_copy(out=v["sq"][k][:, :128], in_=psb[pi][:, :128]),
                          r=[PSK(pi)], w=["lsq%d" % k])
                    row = t0 - 128 + tti * 128
                    dma("pool", out[row:row + 128, dc * 128:(dc + 1) * 128], v["sq"][k][:, :128], ["lsq%d" % k], [("out", b)])
            if not final_out:
                dma("pool", hT[dc * 128:(dc + 1) * 128, t0:t0 + T], v["ho"][hi][:, :T], ["lho%d" % hi], [("hT", b)])
                xi = rotv("lxo", 2)
                act(v["xo"][xi][:, :T], v["yp"][:, dc, :T], AF.Identity, [yk, "fus"], ["lxo%d" % xi],
                    scale=fus[:, r_, 0, dc:dc + 1], bias=fus[:, r_, 1, dc:dc + 1])
                dma("pool", xmT[dc * 128:(dc + 1) * 128, t0:t0 + T], v["xo"][xi][:, :T], ["lxo%d" % xi], [("xmT", b)])

    def phase_C1(l, last):
        AB.off = W_END_B
        AFp.reset()
        v = ln_setup()
        yb = [AB.view("yb%d" % n, 8, 512) for n in range(3)]
        mg = AB.view("mg", 16, 512, sub=True)
        gat = [AB.view("gat%d" % i, 4, 512) for i in range(2)]
        acc = AFp.view("acc", 4, 512, sub=True)
        tmp = [AFp.view("ctmp%d" % i, 512) for i in range(2)]
        srcs = [(yaT, "yaT"), (ysT, "ysT"), (ydT, "ydT")]
        for b, (t0, T) in enumerate(blocks):
            if last and b == 0:
                continue
            r_ = 0 if b > 0 else 1
            fused_affine(r_, VL1G, VL1B, l % 2, DA2, DB2)
            for n in range(3):
                dma("sp", yb[n][:, :, :T], srcs[n][0][:, t0:t0 + T].rearrange("(kc p) t -> p kc t", p=128),
                    [(srcs[n][1], b)], ["yb%d" % n])
            for cg in range(4):
                for n in range(3):
                    wt, wk = wtile()
                    wv = wview(wk, 8, 512)
                    dma("sp", wv, wb_br[l][n * 1024:(n + 1) * 1024, cg * 512:(cg + 1) * 512].rearrange("(kc p) c -> p kc c", p=128),
                        WK("br", l), [wk])
                    gi = rotv("gat", 2)
                    dma("sp", gat[gi][:, :, :T],
                        gT[n * D + cg * 512:n * D + (cg + 1) * 512, t0:t0 + T].rearrange("(c p) t -> p c t", p=128),
                        [("gT", b)], ["gat%d" % gi])
                    for c in range(4):
                        dc = cg * 4 + c
                        pi = ps_next(6)
                        for kc in range(8):
                            mm(pi, 128, T, wv[:, kc, c * 128:(c + 1) * 128], yb[n][:, kc, :T], kc == 0, kc == 7, [wk, "yb%d" % n])
                        if n == 0:
                            tt("dve", acc[:, c, :T], psb[pi][:, :T], gat[gi][:, c, :T], ALU.mult, [PSK(pi), "gat%d" % gi], [("acc", c)])
                        else:
                            k = rotv("ctmp", 2)
                            tt("dve", tmp[k][:, :T], psb[pi][:, :T], gat[gi][:, c, :T], ALU.mult, [PSK(pi), "gat%d" % gi], ["ctmp%d" % k])
                            if n == 1:
                                tt("pool", acc[:, c, :T], acc[:, c, :T], tmp[k][:, :T], ALU.add, [("acc", c), "ctmp%d" % k], [("acc", c)])
                            else:
                                tt("pool", mg[:, dc, :T], acc[:, c, :T], tmp[k][:, :T], ALU.add, [("acc", c), "ctmp%d" % k], [("mg", dc)])
            pend = []

            def ev_o(c, m, pi):
                dc = c // 128
                ln_chunk_in(v, dc, pi, T, b, t0, DG1, r_, l % 2)
                pend.append(dc)
                if len(pend) > 1:
                    ln_stats(v, pend.pop(0), T)
            linear_fm(mg, "mg", 16, wb_out[l], WK("out", l), [(i * 128, 128) for i in range(16)], T, ev_o, pool=6)
            while pend:
                ln_stats(v, pend.pop(0), T)
            ln_finish(v, T, b, t0, r_, l, 1, False)

    def phase_C2(l, last):
        AB.off = X_END_B
        AFp.reset()
        v = ln_setup()
        hff = AB.view("hff", 44, 512, sub=True)
        sg = [AFp.view("sg%d" % i, 512) for i in range(2)]
        for b, (t0, T) in enumerate(blocks):
            if last and b == 0:
                continue
            r_ = 0 if b > 0 else 1
            if not last:
                fused_affine(r_, VL2G, VL2B, (l + 1) % 2, DA1, DB1)
            dma("sp", xm[:, :, :T], xmT[:, t0:t0 + T].rearrange("(kc p) t -> p kc t", p=128), [("xmT", b)], ["xm"])
            for g0 in range(0, FFN, 512):
                wtg, wkg = wtile()
                dma("sp", wtg[:, :, :], wb_gu[l][:, g0:g0 + 512].rearrange("(kc p) c -> p kc c", p=128), WK("gu", l), [wkg])
                wtu, wku = wtile()
                dma("sp", wtu[:, :, :], wb_gu[l][:, FFN + g0:FFN + g0 + 512].rearrange("(kc p) c -> p kc c", p=128), WK("gu", l), [wku])
                for c in range(4):
                    j = g0 // 128 + c
                    pg, pu = ps_next(6), ps_next(6)
                    for kc in range(16):
                        mm(pg, 128, T, wtg[:, kc, c * 128:(c + 1) * 128], xm[:, kc, :T], kc == 0, kc == 15, [wkg, "xm"])
                    for kc in range(16):
                        mm(pu, 128, T, wtu[:, kc, c * 128:(c + 1) * 128], xm[:, kc, :T], kc == 0, kc == 15, [wku, "xm"])
                    si = rotv("sg", 2)
                    act(sg[si][:, :T], psb[pg][:, :T], AF.Silu, [PSK(pg)], ["sg%d" % si])
                    tt("dve", hff[:, j, :T], sg[si][:, :T], psb[pu][:, :T], ALU.mult, ["sg%d" % si, PSK(pu)], [("hff", j)])
            pend = []
            for cg in range(4):
                banks = [ps_next(6) for _ in range(4)]
                for (k0, kn) in ((0, 16), (16, 16), (32, 12)):
                    wt, wk = wtile()
                    dma("sp", wt[:, :kn, :], wb_dn[l][k0 * 128:(k0 + kn) * 128, cg * 512:(cg + 1) * 512].rearrange(
                        "(kc p) c -> p kc c", p=128), WK("dn", l), [wk])
                    for c in range(4):
                        for kc in range(kn):
                            mm(banks[c], 128, T, wt[:, kc, c * 128:(c + 1) * 128], hff[:, k0 + kc, :T],
                               k0 + kc == 0, k0 + kc == 43, [wk, "hff"])
                for c in range(4):
                    dc = cg * 4 + c
                    ln_chunk_in(v, dc, banks[c], T, b, t0, DG2, r_, l % 2)
                    pend.append(dc)
                    if len(pend) > 1:
                        ln_stats(v, pend.pop(0), T)
            while pend:
                ln_stats(v, pend.pop(0), T)
            ln_finish(v, T, b, t0, r_, l, 2, last)

    def phase_in():
        AB.off = X_END_B
        AFp.reset()
        xt = [AFp.view("xt%d" % i, 2048) for i in range(2)]
        ho = [AFp.view("iho%d" % i, 512) for i in range(2)]
        xo = [AB.view("ixo%d" % i, 512) for i in range(2)]
        for b, (t0, T) in enumerate(blocks):
            r_ = 0 if b > 0 else 1
            for tti in range(T // 128):
                xi = rotv("xt", 2)
                dma("sp", xt[xi][:, :], xin[t0 + tti * 128:t0 + (tti + 1) * 128, :], [], ["xt%d" % xi])
                for dc in range(16):
                    pi = ps_next()
                    P.add("pe", lambda e, pi=pi, xi=xi, dc=dc: e.transpose(out=psb[pi][:, :128], in_=xt[xi][:, dc * 128:(dc + 1) * 128],
                                                                           identity=identf[:]),
                          r=["xt%d" % xi, "identf"], w=[PSK(pi)])
                    hi, oi = rotv("iho", 2), rotv("ixo", 2)
                    P.add("dve", lambda e, pi=pi, hi=hi: e.tensor_copy(out=ho[hi][:, :128], in_=psb[pi][:, :128]), r=[PSK(pi)], w=["iho%d" % hi])
                    c0 = t0 + tti * 128
                    dma("pool", hT[dc * 128:(dc + 1) * 128, c0:c0 + 128], ho[hi][:, :128], ["iho%d" % hi], [("hT", b)])
                    act(xo[oi][:, :128], psb[pi][:, :128], AF.Identity, [PSK(pi), "der0"], ["ixo%d" % oi],
                        scale=der[:, r_, 0, DA1, dc:dc + 1], bias=der[:, r_, 0, DB1, dc:dc + 1])
                    dma("pool", xmT[dc * 128:(dc + 1) * 128, c0:c0 + 128], xo[oi][:, :128], ["ixo%d" % oi], [("xmT", b)])

    gather_layer(0, ["ada", "in", "uq", "ukv", "br", "out", "gu", "dn"])
    layer_mod(0)
    layer_vectors(0)
    phase_in()
    for l in range(L):
        last = (l == L - 1)
        phase_A(l, last)
        if not last:
            gather_layer(l + 1, ["ada", "in", "uq", "ukv", "br", "out", "gu", "dn"])
        phase_B(l, last)
        phase_C1(l, last)
        if not last:
            layer_mod(l + 1)
        phase_C2(l, last)
        if not last:
            layer_vectors(l + 1)
    for name, ap_ in dbg.items():
        src = {"hT": hT, "xmT": xmT, "qT": qT, "qdT": qdT, "gT": gT, "ysT": ysT, "yaT": yaT, "ydT": ydT}.get(name)
        if src is None:
            kind_, bi = name.split("_")
            src = {"exf": exf_loc, "ext": ext_loc, "exa": exf_a, "exb": exf_b, "eta": ext_all}[kind_][int(bi)].ap()
        keys = list(P.last_w.keys())
        n0 = src.shape[0]
        for r0 in range(0, n0, 1024):
            r1 = min(n0, r0 + 1024)
            dma("pool", ap_[r0:r1, :], src[r0:r1, :], keys, [])
    P.emit(sems)
    es.close()
    return nc, P


def _rope_tables(n_lat_total, dim, tok_idx):
    rows = n_lat_total // 64
    r = np.repeat(np.arange(rows, dtype=np.float32), 64)
    col = np.tile(np.arange(64, dtype=np.float32), rows)
    quarter = dim // 4
    inv = (np.float32(10000.0) ** (-np.arange(quarter, dtype=np.float32) / np.float32(quarter))).astype(np.float32)
    ar = r[:, None] * inv
    ac = col[:, None] * inv
    ang = np.concatenate([ar, ar, ac, ac], axis=-1)[tok_idx]
    return np.cos(ang).astype(np.float32).T, np.sin(ang).astype(np.float32).T


def _rot_mat(dim):
    q = dim // 4
    m = np.zeros((128, 128), np.float32)
    for a in range(2):
        for i in range(q):
            i0 = a * 2 * q + i
            i1 = a * 2 * q + q + i
            m[i1, i0] = -1.0
            m[i0, i1] = 1.0
    return m


_CACHE = {}


def run(inputs, L, debug=()):
    x = np.asarray(inputs["x"], np.float32)
    B, S, _ = x.shape
    half = S // 2
    NLB = half // 512
    key = (NLB, L, tuple(debug))
    if key not in _CACHE:
        _CACHE[key] = build(NLB, L, debug)
    nc, P = _CACHE[key]
    NTOK = 128 + half
    cmat = np.stack([np.eye(128, dtype=np.float32), _rot_mat(64), _rot_mat(128), np.ones((128, 128), np.float32)])
    names = ["ada_b", "mla_q_norm", "mla_kv_norm", "sgu_ln_g", "sgu_ln_b",
             "sgu_w", "diff_lam", "diff_subln", "ln1_g", "ln1_b", "ln2_g", "ln2_b"]
    shared = {n: np.ascontiguousarray(np.asarray(inputs[n], np.float32)[:L]) for n in names}
    shared["sgu_b"] = np.ascontiguousarray(np.asarray(inputs["sgu_b"], np.float32)[:L].reshape(L, 1024))
    shared["cmat"] = cmat
    shards = [dict() for _ in range(4)]
    for wn, (iname, K_, N_, R_) in WSPEC.items():
        w = np.asarray(inputs[iname], np.float32)[:L].reshape(L, K_ // (4 * R_), 4, R_, N_)
        for r4 in range(4):
            shards[r4][iname] = np.ascontiguousarray(w[:, :, r4].reshape(L, K_ // 4, N_))
    in_maps = []
    for core in range(8):
        b, p = core // 2, core % 2
        m = dict(shared)
        m.update(shards[core % 4])
        m["xin"] = np.concatenate([np.asarray(inputs["ctx"], np.float32)[b, p * 128:(p + 1) * 128],
                                   x[b, p * half:(p + 1) * half]], axis=0)
        m["cvec"] = np.stack([np.asarray(inputs["c"], np.float32)[b], np.asarray(inputs["c_ctx"], np.float32)])
        idx = np.arange(p * half, (p + 1) * half)
        for nm, dim in (("ropeM", 64), ("ropeD", 128)):
            cs, sn = _rope_tables(S, dim, idx)
            cs = np.concatenate([np.ones((dim, 128), np.float32), cs], axis=1)
            sn = np.concatenate([np.zeros((dim, 128), np.float32), sn], axis=1)
            m[nm] = np.ascontiguousarray(np.stack([cs, sn]))
        in_maps.append(m)
    res = run_bass_kernel_spmd(nc, in_maps, core_ids=list(range(8)))
    outp = np.empty((B, S, D), np.float32)
    for core in range(8):
        b, p = core // 2, core % 2
        outp[b, p * half:(p + 1) * half] = res.results[core]["out"]
    return outp, res


def kernel(**inputs):
    outp, _ = run(inputs, 4)
    return outp
```

```python
import math
from contextlib import ExitStack

import numpy as np
import concourse.bass as bass
import concourse.mybir as mybir
from concourse.bass_utils import run_bass_kernel_spmd

F32 = mybir.dt.float32
BF16 = mybir.dt.bfloat16
AF = mybir.ActivationFunctionType
ALU = mybir.AluOpType

D = 2048
KC = 16
IN_W = 12352
FFN = 5632
EPS = 1e-5
ALPHA = 8.0 ** 0.25
C_ZQ, C_ZKV, C_ZKR, C_SGU, C_SGV, C_DQ, C_DK, C_DV, C_GATE = 0, 512, 1024, 1088, 2112, 3136, 4160, 5184, 6208
PAIRS = [[0, 1], [2, 3], [4, 5], [6, 7]]
QUADS = [[0, 1, 2, 3], [4, 5, 6, 7]]
BG_GAP_US = 70.0
WSPEC = {
    "ada": ("ada_w", 2048, 12288, 32),
    "in": ("w_in", 2048, 12352, 32),
    "uq": ("mla_w_uq", 512, 1536, 128),
    "ukv": ("mla_w_ukv", 512, 2048, 128),
    "br": ("w_branch", 3072, 2048, 256),
    "out": ("w_out", 2048, 2048, 256),
    "gu": ("ffn_w_gu", 2048, 11264, 32),
    "dn": ("ffn_w_down", 5632, 2048, 176),
}

COMPUTE = ("pe", "act", "dve", "pool")
DMA_POOLS = {"sp": 24, "pool": 16, "act": 4, "cc": 6}


class Op:
    __slots__ = ("eng", "fn", "kind", "deps", "signaled", "sem", "val", "prewait")


class Prog:
    def __init__(self, nc):
        self.nc = nc
        self.ops = []
        self.last_w = {}
        self.readers = {}
        self.regions = {}
        self.overl = {}

    def region(self, key, buf, lo, hi):
        ov = []
        for k2, (b2, lo2, hi2) in self.regions.items():
            if b2 == buf and lo < hi2 and lo2 < hi:
                ov.append(k2)
                self.overl[k2].append(key)
        self.regions[key] = (buf, lo, hi)
        self.overl[key] = ov

    def add(self, eng, fn, r=(), w=(), kind="c"):
        op = Op()
        op.eng, op.fn, op.kind = eng, fn, kind
        op.deps = set()
        op.signaled = False
        op.sem = None
        op.val = 0
        op.prewait = None
        wx = []
        for k in w:
            wx.append(k)
            for k2 in self.overl.get(k, ()):
                wx.append(k2)
        for k in r:
            p = self.last_w.get(k)
            if p is not None:
                op.deps.add(p)
        for k in wx:
            p = self.last_w.get(k)
            if p is not None:
                op.deps.add(p)
            rd = self.readers.get(k)
            if rd is not None:
                for q in rd[0].values():
                    op.deps.add(q)
                for q in rd[1]:
                    op.deps.add(q)
        for k in wx:
            self.last_w[k] = op
            self.readers[k] = ({}, [])
        for k in r:
            rd = self.readers.setdefault(k, ({}, []))
            if kind == "c":
                rd[0][eng] = op
            else:
                rd[1].append(op)
        op.deps.discard(op)
        if kind == "c" and eng == "pe":
            op.deps = {p for p in op.deps if not (p.kind == "c" and p.eng == "pe")}
        for p in op.deps:
            p.signaled = True
        self.ops.append(op)
        return op

    def emit(self, sems):
        nc = self.nc
        cnt = {e: 0 for e in COMPUTE}
        dcnt = {q: 0 for q in DMA_POOLS}
        for op in self.ops:
            if op.kind == "c":
                if op.signaled:
                    cnt[op.eng] += 1
                    op.sem = sems[op.eng]
                    op.val = cnt[op.eng]
            else:
                q = "cc" if op.kind == "cc" else op.eng
                n = DMA_POOLS[q]
                i = dcnt[q]
                dcnt[q] += 1
                inc = 1 if op.kind == "cc" else 16
                op.sem = sems["%s%d" % (q, i % n)]
                op.val = inc * (i // n + 1)
                if i >= n:
                    op.prewait = (op.sem, inc * (i // n))
        per_eng = {e: [] for e in ("pe", "act", "dve", "pool", "sp")}
        for op in self.ops:
            per_eng[op.eng].append(op)
        self.stats = {e: len(v) for e, v in per_eng.items()}

        def run(eng_name, e):
            waited = {}
            for op in per_eng[eng_name]:
                need = {}
                if op.prewait is not None:
                    need[id(op.prewait[0])] = op.prewait
                for p in op.deps:
                    k = id(p.sem)
                    if k not in need or need[k][1] < p.val:
                        need[k] = (p.sem, p.val)
                for k, (sem, val) in need.items():
                    if waited.get(k, 0) < val:
                        e.wait_ge(sem, val)
                        waited[k] = val
                ins = op.fn(e)
                if op.kind == "d":
                    ins.then_inc(op.sem, 16)
                elif op.kind == "cc":
                    ins.then_inc(op.sem)
                elif op.signaled:
                    ins.then_inc(op.sem, 1)
            last = {}
            for op in per_eng[eng_name]:
                if op.kind != "c":
                    last[id(op.sem)] = (op.sem, op.val)
            for k, (sem, val) in last.items():
                if waited.get(k, 0) < val:
                    e.wait_ge(sem, val)

        with nc.Block() as block:
            @block.tensor
            def _(e):
                run("pe", e)

            @block.scalar
            def _(e):
                run("act", e)

            @block.vector
            def _(e):
                run("dve", e)

            @block.gpsimd
            def _(e):
                run("pool", e)

            @block.sync
            def _(e):
                run("sp", e)


def sem_names():
    names = list(COMPUTE)
    for q, n in DMA_POOLS.items():
        names += ["%s%d" % (q, i) for i in range(n)]
    return names


def build(NLB, L, debug=()):
    NTOK = 128 + NLB * 512
    blocks = [(0, 128)] + [(128 + i * 512, 512) for i in range(NLB)]
    NB = len(blocks)
    NKT = 2 + NLB * 8
    NKEY = NKT * 128
    nc = bass.Bass("TRN2", target_bir_lowering=False)
    P = Prog(nc)
    es = ExitStack()

    def din(name, shape, dt=F32):
        return nc.dram_tensor(name, list(shape), dt, kind="ExternalInput").ap()

    def dscr(name, shape, dt=BF16):
        return nc.dram_tensor(name, list(shape), dt)

    xin = din("xin", [NTOK, D])
    cvec = din("cvec", [2, D])
    ropeM = din("ropeM", [2, 64, NTOK])
    ropeD = din("ropeD", [2, 128, NTOK])
    cmat = din("cmat", [4, 128, 128])
    ada_b = din("ada_b", [L, 6 * D])
    q_norm = din("mla_q_norm", [L, 512])
    kv_norm = din("mla_kv_norm", [L, 512])
    sgu_ln_g = din("sgu_ln_g", [L, 1024])
    sgu_ln_b = din("sgu_ln_b", [L, 1024])
    sgu_w = din("sgu_w", [L, 8, 128, 128])
    sgu_b = din("sgu_b", [L, 1024])
    diff_lam = din("diff_lam", [L, 4, 128])
    diff_subln = din("diff_subln", [L, 256])
    ln1_g = din("ln1_g", [L, D])
    ln1_b = din("ln1_b", [L, D])
    ln2_g = din("ln2_g", [L, D])
    ln2_b = din("ln2_b", [L, D])
    out = nc.dram_tensor("out", [NLB * 512, D], F32, kind="ExternalOutput").ap()

    wsh, wsb, wbt = {}, {}, {}
    for wn, (iname, K_, N_, R_) in WSPEC.items():
        wsh[wn] = din(iname, [L, K_ // 4, N_])
        wsb[wn] = [dscr("ws_%s%d" % (wn, l), [K_ // 4, N_]) for l in range(L)]
        wbt[wn] = [dscr("wb_%s%d" % (wn, l), [K_, N_]) for l in range(L)]
    wb_in = [t.ap() for t in wbt["in"]]
    wb_uq = [t.ap() for t in wbt["uq"]]
    wb_ukv = [t.ap() for t in wbt["ukv"]]
    wb_br = [t.ap() for t in wbt["br"]]
    wb_out = [t.ap() for t in wbt["out"]]
    wb_gu = [t.ap() for t in wbt["gu"]]
    wb_dn = [t.ap() for t in wbt["dn"]]
    wb_ada = [t.ap() for t in wbt["ada"]]

    def WK(wn, l):
        K_, R_ = WSPEC[wn][1], WSPEC[wn][3]
        return [("wb", wn, l, c) for c in range(K_ // (4 * R_))]
    hT = dscr("hT", [D, NTOK], F32).ap()
    xmT = dscr("xmT", [D, NTOK]).ap()
    qT = dscr("qT", [1536, NTOK]).ap()
    qdT = dscr("qdT", [1024, NTOK]).ap()
    gT = dscr("gT", [3 * D, NTOK]).ap()
    ysT = dscr("ysT", [1024, NTOK]).ap()
    yaT = dscr("yaT", [1024, NTOK]).ap()
    ydT = dscr("ydT", [1024, NTOK]).ap()
    exf_loc = [dscr("exf_loc%d" % b, [2112, T]) for b, (t0, T) in enumerate(blocks)]
    exf_a = [dscr("exf_a%d" % b, [2 * 1024, T]) for b, (t0, T) in enumerate(blocks)]
    exf_b = [dscr("exf_b%d" % b, [2 * 1088, T]) for b, (t0, T) in enumerate(blocks)]
    ext_loc = [dscr("ext_loc%d" % b, [T, 2048]) for b, (t0, T) in enumerate(blocks)]
    ext_all = [dscr("ext_all%d" % b, [2 * T, 2048]) for b, (t0, T) in enumerate(blocks)]
    dbg = {}
    for name, shape, dt in debug:
        dbg[name] = nc.dram_tensor("dbg_" + name, list(shape), dt, kind="ExternalOutput").ap()

    NBF = 49152
    NFP = 16384
    arB = es.enter_context(nc.sbuf_tensor("arB", [128, NBF], BF16))
    arF = es.enter_context(nc.sbuf_tensor("arF", [128, NFP], F32))
    consts = {}

    def cst(name, shape, dt):
        consts[name] = es.enter_context(nc.sbuf_tensor(name, shape, dt))
        return consts[name]

    identf = cst("identf", [128, 128], F32)
    identb = cst("identb", [128, 128], BF16)
    r64b = cst("r64b", [128, 128], BF16)
    r128b = cst("r128b", [128, 128], BF16)
    onesf = cst("onesf", [128, 128], F32)
    onesb = cst("onesb", [128, 128], BF16)
    cstage = cst("cstage", [128, 128], F32)
    vecT = cst("vecT", [128, 80], F32)
    modT = cst("modT", [128, 2, 96], F32)
    der = cst("der", [128, 2, 2, 8, 16], F32)
    smalls = cst("smalls", [128, 16], F32)
    fus = cst("fus", [128, 2, 2, 16], F32)
    scT = cst("scT", [128, 16, 2], BF16)
    wsT = cst("wsT", [128, 8, 128], BF16)
    bsB = cst("bsB", [128, 1024], F32)
    sgBg = cst("sgBg", [128, 1024], F32)
    sgBb = cst("sgBb", [128, 1024], F32)
    psb = [es.enter_context(nc.psum_tensor("ps%d" % i, [128, 512], F32)) for i in range(8)]
    psbf = None
    sems = {n: es.enter_context(nc.semaphore(n)) for n in sem_names()}

    class Arena:
        def __init__(self, t, name, n):
            self.t, self.name, self.n, self.off = t, name, n, 0
            self.flat = {}

        def reset(self):
            self.off = 0

        def _reg(self, key, lo, n):
            if key not in P.regions:
                P.region(key, self.name, lo, lo + n)
            else:
                assert P.regions[key] == (self.name, lo, lo + n), (key, P.regions[key], lo, n)

        def view(self, key, *shape, at=None, sub=False):
            n = 1
            for s in shape:
                n *= s
            if at is None:
                lo = self.off
                self.off += n
            else:
                lo = at
            assert lo + n <= self.n, (key, lo, n, self.n)
            self._reg(key, lo, n)
            if sub:
                m = n // shape[0]
                for i in range(shape[0]):
                    self._reg((key, i), lo + i * m, m)
            v = self.t[:, lo:lo + n]
            self.flat[key] = v
            if len(shape) == 2:
                v = v.rearrange("p (a b) -> p a b", b=shape[1])
            elif len(shape) == 3:
                v = v.rearrange("p (a b c) -> p a b c", b=shape[1], c=shape[2])
            return v

    AB = Arena(arB, "arB", NBF)
    AFp = Arena(arF, "arF", NFP)

    psctr = [0]

    def ps_next(pool=8):
        i = psctr[0] % pool
        psctr[0] += 1
        return i

    def PSK(i):
        return "ps%d" % i

    rot = {}

    def rotv(name, n):
        i = rot.get(name, 0)
        rot[name] = i + 1
        return i % n

    def dma(q, out_ap, in_ap, r, w):
        P.add(q, lambda e: e.dma_start(out=out_ap, in_=in_ap), r=r, w=w, kind="d")

    bg = []
    pe_us = [0.0, 0.0]

    def bg_tick(us):
        pe_us[0] += us
        if bg and pe_us[0] - pe_us[1] >= BG_GAP_US:
            pe_us[1] = pe_us[0]
            bg.pop(0)[1]()

    def bg_flush(tag=None):
        while bg and (tag is None or any(t == tag for t, _ in bg)):
            bg.pop(0)[1]()

    def mm(ps_i, rows, cols, lhsT, rhs, start, stop, r):
        P.add("pe", lambda e: e.matmul(psb[ps_i][:rows, :cols], lhsT=lhsT, rhs=rhs, start=start, stop=stop),
              r=r, w=[PSK(ps_i)])
        bg_tick(max(cols, 64) / 2000.0)

    def act(out_ap, in_ap, func, r, w, scale=None, bias=None):
        kw = {}
        if scale is not None:
            kw["scale"] = scale
        if bias is not None:
            kw["bias"] = bias
        P.add("act", lambda e: e.activation(out=out_ap, in_=in_ap, func=func, **kw), r=r, w=w)

    def tt(eng, out_ap, a, b, op, r, w):
        P.add(eng, lambda e: e.tensor_tensor(out=out_ap, in0=a, in1=b, op=op), r=r, w=w)

    def ts(eng, out_ap, a, s1, s2, op0, op1, r, w):
        P.add(eng, lambda e: e.tensor_scalar(out=out_ap, in0=a, scalar1=s1, scalar2=s2, op0=op0, op1=op1), r=r, w=w)

    def stt(eng, out_ap, a, scalar, b, op0, op1, r, w):
        P.add(eng, lambda e: e.scalar_tensor_tensor(out=out_ap, in0=a, scalar=scalar, in1=b, op0=op0, op1=op1), r=r, w=w)

    def rsqrt_chain(dst, dst_key, src, src_key, mult, add):
        ts("dve", dst, src, mult, add, ALU.mult, ALU.add, [src_key], [dst_key])
        act(dst, dst, AF.Sqrt, [dst_key], [dst_key])
        P.add("dve", lambda e: e.reciprocal(out=dst, in_=dst), r=[dst_key], w=[dst_key])

    dma("sp", identf[:], cmat[0], [], ["identf"])
    P.add("dve", lambda e: e.tensor_copy(out=identb[:], in_=identf[:]), r=["identf"], w=["identb"])
    dma("sp", cstage[:], cmat[1], [], ["cstage"])
    P.add("dve", lambda e: e.tensor_copy(out=r64b[:], in_=cstage[:]), r=["cstage"], w=["r64b"])
    dma("sp", cstage[:], cmat[2], ["cstage"], ["cstage"])
    P.add("dve", lambda e: e.tensor_copy(out=r128b[:], in_=cstage[:]), r=["cstage"], w=["r128b"])
    dma("sp", onesf[:], cmat[3], [], ["onesf"])
    P.add("dve", lambda e: e.tensor_copy(out=onesb[:], in_=onesf[:]), r=["onesf"], w=["onesb"])

    dma("sp", cstage[0:16, :], cvec[0].rearrange("(c p) -> c p", p=128), ["cstage"], ["cstage"])
    dma("sp", cstage[16:32, :], cvec[1].rearrange("(c p) -> c p", p=128), ["cstage"], ["cstage"])
    pi = ps_next()
    P.add("pe", lambda e: e.transpose(out=psb[pi][:, :32], in_=cstage[0:32, :], identity=identf[0:32, 0:32]),
          r=["cstage", "identf"], w=[PSK(pi)])
    for r_ in range(2):
        act(scT[:, :, r_], psb[pi][:, r_ * 16:(r_ + 1) * 16], AF.Silu, [PSK(pi)], ["scT"])

    def gather_layer(l, names, background=False):
        items = []
        for wn in names:
            iname, K_, N_, R_ = WSPEC[wn]
            skeys = []
            for c0 in range(0, N_, 2048):
                c1 = min(N_, c0 + 2048)
                sk = ("ws", wn, l, c0)
                skeys.append(sk)
                items.append(((wn, l), lambda wn=wn, c0=c0, c1=c1, sk=sk: dma(
                    "pool", wsb[wn][l].ap()[:, c0:c1], wsh[wn][l][:, c0:c1], [], [sk])))
            for c in range(K_ // (4 * R_)):
                sa = wsb[wn][l].ap()[c * R_:(c + 1) * R_, :]
                da = wbt[wn][l].ap()[c * 4 * R_:(c + 1) * 4 * R_, :]
                items.append(((wn, l), lambda sa=sa, da=da, skeys=list(skeys), wn=wn, c=c: P.add(
                    "pool", lambda e: e.collective_compute(
                        "AllGather", ALU.bypass, replica_groups=QUADS, ins=[sa.opt()], outs=[da.opt()]),
                    r=skeys, w=[("wb", wn, l, c)], kind="cc")))
        if background:
            bg.extend(items)
        else:
            for _, it in items:
                it()

    def wtile():
        i = rotv("wt", 2)
        return AB_w[i], "wt%d" % i

    def wview(wk, KCn, gc):
        return AB.flat[wk][:, 0:KCn * gc].rearrange("p (a b) -> p a b", b=gc)

    VQ, VKV, VL1G, VL1B, VL2G, VL2B, VSUB, VLAM = 0, 4, 8, 24, 40, 56, 72, 74
    DA1, DB1, DG1, DA2, DB2, DG2 = 0, 1, 2, 3, 4, 5
    DLG, DLB = 6, 7

    def layer_vectors(l):
        lam_init = 0.8 - 0.6 * math.exp(-0.3 * l)
        rows = [(q_norm[l], 4, VQ), (kv_norm[l], 4, VKV), (ln1_g[l], 16, VL1G), (ln1_b[l], 16, VL1B),
                (ln2_g[l], 16, VL2G), (ln2_b[l], 16, VL2B), (diff_subln[l], 2, VSUB)]
        for src, n, r0 in rows:
            dma("sp", cstage[r0:r0 + n, :], src.rearrange("(c p) -> c p", p=128), ["cstage"], ["cstage"])
        dma("sp", cstage[VLAM:VLAM + 4, :], diff_lam[l], ["cstage"], ["cstage"])
        pi = ps_next()
        P.add("pe", lambda e: e.transpose(out=psb[pi][:, :78], in_=cstage[0:78, :], identity=identf[0:78, 0:78]),
              r=["cstage", "identf"], w=[PSK(pi)])
        P.add("dve", lambda e: e.tensor_copy(out=vecT[:, 0:78], in_=psb[pi][:, :78]), r=[PSK(pi)], w=["vecT"])
        tt("dve", smalls[:, 0:1], vecT[:, VLAM:VLAM + 1], vecT[:, VLAM + 1:VLAM + 2], ALU.mult, ["vecT"], ["smalls"])
        tt("dve", smalls[:, 1:2], vecT[:, VLAM + 2:VLAM + 3], vecT[:, VLAM + 3:VLAM + 4], ALU.mult, ["vecT"], ["smalls"])
        pj = ps_next()
        mm(pj, 128, 2, onesf[:], smalls[:, 0:2], True, True, ["onesf", "smalls"])
        act(smalls[:, 2:4], psb[pj][:, 0:2], AF.Exp, [PSK(pj)], ["smalls"])
        tt("dve", smalls[:, 4:5], smalls[:, 3:4], smalls[:, 2:3], ALU.subtract, ["smalls"], ["smalls"])
        ts("dve", smalls[:, 5:6], smalls[:, 4:5], 1.0, -lam_init, ALU.mult, ALU.add, ["smalls"], ["smalls"])
        ts("dve", smalls[:, 6:8], vecT[:, VSUB:VSUB + 2], (1.0 - lam_init), 0.0, ALU.mult, ALU.add, ["vecT"], ["smalls"])
        dma("sp", bsB[:], sgu_b[l].partition_broadcast(128), [], ["bsB"])
        dma("sp", sgBg[:], sgu_ln_g[l].partition_broadcast(128), [], ["sgBg"])
        dma("sp", sgBb[:], sgu_ln_b[l].partition_broadcast(128), [], ["sgBb"])
        for g in range(8):
            dma("sp", cstage[:], sgu_w[l, g], ["cstage"], ["cstage"])
            pk = ps_next()
            P.add("pe", lambda e, pk=pk: e.transpose(out=psb[pk][:, :128], in_=cstage[:], identity=identf[:]),
                  r=["cstage", "identf"], w=[PSK(pk)])
            P.add("dve", lambda e, pk=pk, g=g: e.tensor_copy(out=wsT[:, g, :], in_=psb[pk][:, :128]), r=[PSK(pk)], w=["wsT"])

    def layer_mod(l):
        for c0 in range(0, 96, 32):
            dma("sp", cstage[0:32, :], ada_b[l, c0 * 128:(c0 + 32) * 128].rearrange("(c p) -> c p", p=128),
                ["cstage"], ["cstage"])
            pi = ps_next()
            P.add("pe", lambda e, pi=pi: e.transpose(out=psb[pi][:, :32], in_=cstage[0:32, :], identity=identf[0:32, 0:32]),
                  r=["cstage", "identf"], w=[PSK(pi)])
            for r_ in range(2):
                P.add("dve", lambda e, pi=pi, r_=r_, c0=c0: e.tensor_copy(out=modT[:, r_, c0:c0 + 32], in_=psb[pi][:, :32]),
                      r=[PSK(pi)], w=["modT"])
        for g in range(24):
            wt, wk = wtile()
            dma("sp", wt[:, :, :], wb_ada[l][:, g * 512:(g + 1) * 512].rearrange("(kc p) c -> p kc c", p=128), WK("ada", l), [wk])
            pi = ps_next()
            for c in range(4):
                for kc in range(16):
                    P.add("pe", lambda e, pi=pi, c=c, kc=kc, wt=wt: e.matmul(
                        psb[pi][:, 2 * c:2 * c + 2], lhsT=wt[:, kc, c * 128:(c + 1) * 128], rhs=scT[:, kc, :],
                        start=(kc == 0), stop=(kc == 15)), r=[wk, "scT"], w=[PSK(pi)])
            for r_ in range(2):
                ch = g * 4
                P.add("dve", lambda e, pi=pi, r_=r_, ch=ch: e.tensor_tensor(
                    out=modT[:, r_, ch:ch + 4], in0=psb[pi][:, r_:8:2], in1=modT[:, r_, ch:ch + 4], op=ALU.add),
                    r=[PSK(pi), "modT"], w=["modT"])
        lp = l % 2
        dk = "der%d" % lp
        for r_ in range(2):
            ts("dve", der[:, r_, lp, DA1, :], modT[:, r_, 16:32], 1.0, 1.0, ALU.mult, ALU.add, ["modT"], [dk])
            P.add("dve", lambda e, r_=r_: e.tensor_copy(out=der[:, r_, lp, DB1, :], in_=modT[:, r_, 0:16]), r=["modT"], w=[dk])
            ts("dve", der[:, r_, lp, DG1, :], modT[:, r_, 32:48], 1.0 / ALPHA, 0.0, ALU.mult, ALU.add, ["modT"], [dk])
            ts("dve", der[:, r_, lp, DA2, :], modT[:, r_, 64:80], 1.0, 1.0, ALU.mult, ALU.add, ["modT"], [dk])
            P.add("dve", lambda e, r_=r_: e.tensor_copy(out=der[:, r_, lp, DB2, :], in_=modT[:, r_, 48:64]), r=["modT"], w=[dk])
            ts("dve", der[:, r_, lp, DG2, :], modT[:, r_, 80:96], 1.0 / ALPHA, 0.0, ALU.mult, ALU.add, ["modT"], [dk])

    def fused_affine(r_, gcol, bcol, lp, acol, bbcol):
        dk = "der%d" % lp
        tt("dve", fus[:, r_, 0, :], vecT[:, gcol:gcol + 16], der[:, r_, lp, acol, :], ALU.mult, ["vecT", dk], ["fus"])
        tt("dve", fus[:, r_, 1, :], vecT[:, bcol:bcol + 16], der[:, r_, lp, acol, :], ALU.mult, ["vecT", dk], ["fus"])
        tt("dve", fus[:, r_, 1, :], fus[:, r_, 1, :], der[:, r_, lp, bbcol, :], ALU.add, ["fus", dk], ["fus"])

    def linear_fm(xt, xkey, KCn, wsrc, wkey, chunks, T, evac, pool=8):
        gw = 8192 // KCn
        i = 0
        while i < len(chunks):
            g0 = chunks[i][0]
            j = i
            while j < len(chunks) and chunks[j][0] + chunks[j][1] - g0 <= gw:
                j += 1
            gc = chunks[j - 1][0] + chunks[j - 1][1] - g0
            wt, wk = wtile()
            wv = wview(wk, KCn, gc)
            dma("sp", wv, wsrc[:, g0:g0 + gc].rearrange("(kc p) c -> p kc c", p=128), wkey, [wk])
            for (c, m) in chunks[i:j]:
                pi = ps_next(pool)
                for kc in range(KCn):
                    mm(pi, m, T, wv[:, kc, c - g0:c - g0 + m], xt[:, kc, :T], kc == 0, kc == KCn - 1, [wk, xkey])
                evac(c, m, pi)
            i = j

    def linear_tm(xt, xkey, KCn, wsrc, wkey, col0, ncols, T, evac, colsel=None):
        gw = 8192 // KCn
        for g0 in range(col0, col0 + ncols, gw):
            gc = min(gw, col0 + ncols - g0)
            wt, wk = wtile()
            wv = wview(wk, KCn, gc)
            dma("sp", wv, wsrc[:, g0:g0 + gc].rearrange("(kc p) c -> p kc c", p=128), wkey, [wk])
            evac(g0 - col0, gc, wv, wk)

    def rope(src, skey, Dm, T, cosv, sinv, tkey, dst, dkey, t1, t1k, t2, t2k):
        rm = r64b if Dm == 64 else r128b
        rk = "r64b" if Dm == 64 else "r128b"
        pi = ps_next()
        mm(pi, Dm, T, rm[:Dm, :Dm], src, True, True, [rk, skey])
        tt("pool", t1[:Dm, :T], src, cosv[:Dm, :T], ALU.mult, [skey, tkey], [t1k])
        tt("dve", t2[:Dm, :T], psb[pi][:Dm, :T], sinv[:Dm, :T], ALU.mult, [PSK(pi), tkey], [t2k])
        tt("dve", dst, t1[:Dm, :T], t2[:Dm, :T], ALU.add, [t1k, t2k], [dkey])

    def gelu_evac(ps_ap, pkey, T, dst, dkey, g1, g1k, g2, g2k):
        act(g1[:, :T], ps_ap, AF.Square, [pkey], [g1k])
        ts("pool", g1[:, :T], g1[:, :T], 0.044715, 1.0, ALU.mult, ALU.add, [g1k], [g1k])
        tt("dve", g2[:, :T], g1[:, :T], ps_ap, ALU.mult, [g1k, pkey], [g2k])
        act(g2[:, :T], g2[:, :T], AF.Sigmoid, [g2k], [g2k], scale=1.5957691216057308)
        tt("dve", dst, g2[:, :T], ps_ap, ALU.mult, [g2k, pkey], [dkey])

    AB.reset()
    AB_w = [AB.view("wt0", 16, 512), AB.view("wt1", 16, 512)]
    W_END_B = AB.off
    xm = AB.view("xm", 16, 512)
    X_END_B = AB.off

    def phase_A(l, last):
        AB.off = X_END_B
        AFp.reset()
        zg = [AB.view("zqg", 4, 512), AB.view("zkvg", 4, 512)]
        zsq = [AB.view("zqsq", 4, 512), AB.view("zkvsq", 4, 512)]
        zn = zg
        ZG, ZSQ = ["zqg", "zkvg"], ["zqsq", "zkvsq"]
        uT = AB.view("uT", 8, 512)
        vn = AB.view("vn", 1024)
        ysb = AB.view("ysb", 8, 512)
        st = [AB.view("st%d" % i, 512) for i in range(4)]
        so = [AB.view("so%d" % i, 512) for i in range(3)]
        vst = [AB.view("vst%d" % i, 1024) for i in range(2)]
        vt = AFp.view("vt", 4, 1024, sub=True)
        cosM, sinM = AFp.view("cosM", 512), AFp.view("sinM", 512)
        cosD, sinD = AFp.view("cosD", 512), AFp.view("sinD", 512)
        rr = [AFp.view("rq", 512), AFp.view("rkv", 512)]
        t1 = [AFp.view("t1_%d" % i, 512) for i in range(2)]
        t2 = [AFp.view("t2_%d" % i, 512) for i in range(2)]
        gt = [AFp.view("gt%d" % i, 512) for i in range(4)]
        bn = AFp.view("bn", 16)
        for b, (t0, T) in enumerate(blocks):
            r_ = 0 if b > 0 else 1
            ntt = T // 128
            xk = "xm"
            dma("sp", xm[:, :, :T], xmT[:, t0:t0 + T].rearrange("(kc p) t -> p kc t", p=128), [("xmT", b)], [xk])
            dma("sp", cosM[:64, :T], ropeM[0, :, t0:t0 + T], [], ["cosM"])
            dma("sp", sinM[:64, :T], ropeM[1, :, t0:t0 + T], [], ["sinM"])
            dma("sp", cosD[:, :T], ropeD[0, :, t0:t0 + T], [], ["cosD"])
            dma("sp", sinD[:, :T], ropeD[1, :, t0:t0 + T], [], ["sinD"])
            wsrc, wkey = wb_in[l], WK("in", l)
            exf, exk = exf_loc[b].ap(), ("exf_loc", b)
            ext, etk = ext_loc[b].ap(), ("ext_loc", b)

            for zi, (c0, nv) in enumerate(((C_ZQ, VQ), (C_ZKV, VKV))):
                def ev(c, m, pi, zi=zi, c0=c0, nv=nv):
                    ci = (c - c0) // 128
                    act(zsq[zi][:, ci, :T], psb[pi][:, :T], AF.Square, [PSK(pi)], [ZSQ[zi]])
                    act(zg[zi][:, ci, :T], psb[pi][:, :T], AF.Identity, [PSK(pi), "vecT"], [ZG[zi]],
                        scale=vecT[:, nv + ci:nv + ci + 1])
                linear_fm(xm, xk, 16, wsrc, wkey, [(c0 + i * 128, 128) for i in range(4)], T, ev)
                pj = ps_next()
                for ci in range(4):
                    mm(pj, 128, T, onesb[:], zsq[zi][:, ci, :T], ci == 0, ci == 3, ["onesb", ZSQ[zi]])
                rk = "rq" if zi == 0 else "rkv"
                rsqrt_chain(rr[zi][:, :T], rk, psb[pj][:, :T], PSK(pj), 1.0 / 512.0, EPS)
                for ci in range(4):
                    tt("dve" if ci % 2 else "pool", zn[zi][:, ci, :T], zg[zi][:, ci, :T], rr[zi][:, :T], ALU.mult,
                       [ZG[zi], rk], [ZG[zi]])

            def ev_kr(c, m, pi):
                i = rotv("st", 4)
                act(st[i][:64, :T], psb[pi][:64, :T], AF.Copy, [PSK(pi)], ["st%d" % i])
                j, k = rotv("so", 3), rotv("t", 2)
                rope(st[i][:64, :T], "st%d" % i, 64, T, cosM, sinM, "cosM", so[j][:64, :T], "so%d" % j,
                     t1[k], "t1_%d" % k, t2[k], "t2_%d" % k)
                dma("pool", exf[2048:2112, :], so[j][:64, :T], ["so%d" % j], [exk])
            linear_fm(xm, xk, 16, wsrc, wkey, [(C_ZKR, 64)], T, ev_kr)

            if not (last and b == 0):
                def ev_q(c, m, pi):
                    i = rotv("st", 4)
                    if m == 128:
                        h = c // 192
                        act(st[i][:, :T], psb[pi][:, :T], AF.Copy, [PSK(pi)], ["st%d" % i])
                        dma("pool", qT[h * 192:h * 192 + 128, t0:t0 + T], st[i][:, :T], ["st%d" % i], [("qT", b)])
                    else:
                        h = c // 192
                        act(st[i][:64, :T], psb[pi][:64, :T], AF.Copy, [PSK(pi)], ["st%d" % i])
                        j, k = rotv("so", 3), rotv("t", 2)
                        rope(st[i][:64, :T], "st%d" % i, 64, T, cosM, sinM, "cosM", so[j][:64, :T], "so%d" % j,
                             t1[k], "t1_%d" % k, t2[k], "t2_%d" % k)
                        dma("pool", qT[h * 192 + 128:h * 192 + 192, t0:t0 + T], so[j][:64, :T], ["so%d" % j], [("qT", b)])
                chq = []
                for h in range(8):
                    chq += [(h * 192, 128), (h * 192 + 128, 64)]
                linear_fm(zn[0], ZG[0], 4, wb_uq[l], WK("uq", l), chq, T, ev_q)

            def ev_k(c, m, pi):
                h = c // 256
                i = rotv("st", 4)
                act(st[i][:, :T], psb[pi][:, :T], AF.Copy, [PSK(pi)], ["st%d" % i])
                dma("pool", exf[h * 128:(h + 1) * 128, :], st[i][:, :T], ["st%d" % i], [exk])
            wt, wk = wtile()
            wv = wview(wk, 4, 2048)
            dma("sp", wv, wb_ukv[l].rearrange("(kc p) c -> p kc c", p=128), WK("ukv", l), [wk])
            for h in range(8):
                pi = ps_next()
                for kc in range(4):
                    mm(pi, 128, T, wv[:, kc, h * 256:h * 256 + 128], zn[1][:, kc, :T], kc == 0, kc == 3, [wk, ZG[1]])
                ev_k(h * 256, 128, pi)
            wv4 = wv.rearrange("p a (h c) -> p a h c", c=256)
            for tti in range(ntt):
                vi = rotv("vst", 2)
                for hg in range(2):
                    pi = ps_next()
                    for kc in range(4):
                        mm(pi, 128, 512, zn[1][:, kc, tti * 128:(tti + 1) * 128], wv4[:, kc, hg * 4:hg * 4 + 4, 128:256],
                           kc == 0, kc == 3, [wk, ZG[1]])
                    act(vst[vi][:, hg * 512:(hg + 1) * 512], psb[pi][:, :512], AF.Copy, [PSK(pi)], ["vst%d" % vi])
                dma("pool", ext[tti * 128:(tti + 1) * 128, 0:1024], vst[vi][:, :], ["vst%d" % vi], [etk])

            for (c0, isq) in ((C_DQ, True), (C_DK, False)):
                if isq and last and b == 0:
                    continue

                def ev_d(c, m, pi, c0=c0, isq=isq):
                    jh = (c - c0) // 128
                    i = rotv("st", 4)
                    act(st[i][:, :T], psb[pi][:, :T], AF.Copy, [PSK(pi)], ["st%d" % i])
                    j, k = rotv("so", 3), rotv("t", 2)
                    rope(st[i][:, :T], "st%d" % i, 128, T, cosD, sinD, "cosD", so[j][:, :T], "so%d" % j,
                         t1[k], "t1_%d" % k, t2[k], "t2_%d" % k)
                    if isq:
                        dma("pool", qdT[jh * 128:(jh + 1) * 128, t0:t0 + T], so[j][:, :T], ["so%d" % j], [("qdT", b)])
                    else:
                        dma("pool", exf[1024 + jh * 128:1024 + (jh + 1) * 128, :], so[j][:, :T], ["so%d" % j], [exk])
                linear_fm(xm, xk, 16, wsrc, wkey, [(c0 + i * 128, 128) for i in range(8)], T, ev_d)

            def ev_dv(off, gc, wv, wk):
                for tti in range(ntt):
                    pi = ps_next()
                    for kc in range(16):
                        mm(pi, 128, gc, xm[:, kc, tti * 128:(tti + 1) * 128], wv[:, kc, :gc], kc == 0, kc == 15, [wk, xk])
                    vi = rotv("vst", 2)
                    act(vst[vi][:, :gc], psb[pi][:, :gc], AF.Copy, [PSK(pi)], ["vst%d" % vi])
                    dma("pool", ext[tti * 128:(tti + 1) * 128, 1024 + off:1024 + off + gc], vst[vi][:, :gc],
                        ["vst%d" % vi], [etk])
            linear_tm(xm, xk, 16, wsrc, wkey, C_DV, 1024, T, ev_dv)

            def cc(src_t, r0, r1, dst_t, rkey, wkey_):
                sa = src_t.ap()[r0:r1, :]
                P.add("pool", lambda e: e.collective_compute("AllGather", ALU.bypass, replica_groups=PAIRS,
                                                             ins=[sa.opt()], outs=[dst_t.ap().opt()]),
                      r=[rkey], w=[wkey_], kind="cc")
            cc(exf_loc[b], 0, 1024, exf_a[b], exk, ("exf_a", b))
            cc(exf_loc[b], 1024, 2112, exf_b[b], exk, ("exf_b", b))
            cc(ext_loc[b], 0, T, ext_all[b], etk, ("ext_all", b))

            if last and b == 0:
                continue
            def ev_g(c, m, pi):
                i = rotv("st", 4)
                act(st[i][:, :T], psb[pi][:, :T], AF.Sigmoid, [PSK(pi)], ["st%d" % i])
                dma("pool", gT[c - C_GATE:c - C_GATE + 128, t0:t0 + T], st[i][:, :T], ["st%d" % i], [("gT", b)])
            linear_fm(xm, xk, 16, wsrc, wkey, [(C_GATE + i * 128, 128) for i in range(48)], T, ev_g)

            def ev_u(c, m, pi):
                ci = (c - C_SGU) // 128
                a, bb = rotv("gt", 2), rotv("gt2", 2)
                gelu_evac(psb[pi][:, :T], PSK(pi), T, uT[:, ci, :T], "uT", gt[a], "gt%d" % a, gt[2 + bb], "gt%d" % (2 + bb))
            linear_fm(xm, xk, 16, wsrc, wkey, [(C_SGU + i * 128, 128) for i in range(8)], T, ev_u)

            def ev_v(off, gc, wv, wk):
                for tti in range(ntt):
                    pi = ps_next()
                    for kc in range(16):
                        mm(pi, 128, gc, xm[:, kc, tti * 128:(tti + 1) * 128], wv[:, kc, :gc], kc == 0, kc == 15, [wk, xk])
                    a, bb = rotv("gt", 2), rotv("gt2", 2)
                    gelu_evac(psb[pi][:, :gc], PSK(pi), gc, vt[:, tti, off:off + gc], ("vt", tti),
                              gt[a], "gt%d" % a, gt[2 + bb], "gt%d" % (2 + bb))
            linear_tm(xm, xk, 16, wsrc, wkey, C_SGV, 1024, T, ev_v)

            for tti in range(ntt):
                vk = ("vt", tti)
                for hh in range(2):
                    P.add("dve", lambda e, tti=tti, hh=hh: e.bn_stats(out=bn[:, hh * 6:(hh + 1) * 6],
                                                                      in_=vt[:, tti, hh * 512:(hh + 1) * 512]), r=[vk], w=["bn"])
                P.add("dve", lambda e: e.bn_aggr(out=bn[:, 12:14], in_=bn[:, 0:12].rearrange("p (a b) -> p a b", b=6)),
                      r=["bn"], w=["bn"])
                rsqrt_chain(bn[:, 14:15], "bn", bn[:, 13:14], "bn", 1.0, EPS)
                stt("dve", bn[:, 15:16], bn[:, 12:13], -1.0, bn[:, 14:15], ALU.mult, ALU.mult, ["bn"], ["bn"])
                act(vt[:, tti, :], vt[:, tti, :], AF.Identity, [vk, "bn"], [vk], scale=bn[:, 14:15], bias=bn[:, 15:16])
                tt("pool", vt[:, tti, :], vt[:, tti, :], sgBg[:], ALU.mult, [vk, "sgBg"], [vk])
                tt("dve", vn[:, :], vt[:, tti, :], sgBb[:], ALU.add, [vk, "sgBb"], ["vn"])
                for hg in range(2):
                    pi = ps_next()
                    for g4 in range(4):
                        g = hg * 4 + g4
                        P.add("pe", lambda e, pi=pi, g=g, g4=g4: e.matmul(psb[pi][:, g4 * 128:(g4 + 1) * 128],
                                                                           lhsT=vn[:, g * 128:(g + 1) * 128], rhs=wsT[:, g, :],
                                                                           start=True, stop=True),
                              r=["vn", "wsT"], w=[PSK(pi)])
                    k = rotv("t", 2)
                    tt("dve", t2[k][:, :512], psb[pi][:, :512], bsB[:, hg * 512:(hg + 1) * 512], ALU.add,
                       [PSK(pi), "bsB"], ["t2_%d" % k])
                    tt("pool", ysb[:, hg * 4:hg * 4 + 4, tti * 128:(tti + 1) * 128],
                       t2[k][:, :512].rearrange("p (g c) -> p g c", c=128),
                       uT[:, hg * 4:hg * 4 + 4, tti * 128:(tti + 1) * 128], ALU.mult, ["t2_%d" % k, "uT"], ["ysb"])
            dma("pool", ysT[:, t0:t0 + T].rearrange("(g c) t -> c g t", c=128), ysb[:, :, :T], ["ysb"], [("ysT", b)])

    def phase_B(l, last):
        AB.off = 0
        AFp.reset()
        KT = [AB.view("KT%d" % i, NKEY) for i in range(2)]
        KR = AB.view("KR", NKEY)
        VV = [AB.view("VV%d" % i, NKT, 256) for i in range(2)]
        QN = [AB.view("QN%d" % i, 512) for i in range(2)]
        QR = [AB.view("QR%d" % i, 512) for i in range(2)]
        PT = [AB.view("PT%d" % i, 512) for i in range(3)]
        OB = [AB.view("OB%d" % i, 2, 512) for i in range(2)]
        SQ = AB.view("SQd", 2, 512)
        rl = AFp.view("rl", 512)
        om = AFp.view("om", 2, 2, 512)
        dd = AFp.view("dd", 2, 512)
        rd = AFp.view("rdd", 512)
        qblocks = [bb for bb in range(NB) if not (last and bb == 0)]

        def key_pieces():
            res = [(0, 0, 0, 128), (1, 0, 1, 128)]
            i = 2
            for bb in range(1, NB):
                for r_ in range(2):
                    res.append((i, bb, r_, 512))
                    i += 4
            return res
        pieces = key_pieces()
        allA = [("exf_a", bb) for bb in range(NB)]
        allB = [("exf_b", bb) for bb in range(NB)]
        allT = [("ext_all", bb) for bb in range(NB)]

        for (i0, bb, r_, T) in pieces:
            dma("sp", KR[:64, i0 * 128:i0 * 128 + T], exf_b[bb].ap()[r_ * 1088 + 1024:r_ * 1088 + 1088, :], allB, ["KR"])

        def load_head(kind, j):
            ki = rotv("KT", 2)
            for (i0, bb, r_, T) in pieces:
                if kind == "m":
                    src = exf_a[bb].ap()[r_ * 1024 + j * 128:r_ * 1024 + (j + 1) * 128, :]
                    dma("sp", KT[ki][:, i0 * 128:i0 * 128 + T], src, allA, ["KT%d" % ki])
                else:
                    src = exf_b[bb].ap()[r_ * 1088 + j * 128:r_ * 1088 + (j + 1) * 128, :]
                    dma("sp", KT[ki][:, i0 * 128:i0 * 128 + T], src, allB, ["KT%d" % ki])
            return ki

        def load_v(kind, h):
            vi = rotv("VV", 2)
            for (i0, bb, r_, T) in pieces:
                if kind == "m":
                    src = ext_all[bb].ap()[r_ * T:(r_ + 1) * T, h * 128:(h + 1) * 128].rearrange("(t p) c -> p t c", p=128)
                    dma("sp", VV[vi][:, i0:i0 + T // 128, 0:128], src, allT, ["VV%d" % vi])
                else:
                    src = ext_all[bb].ap()[r_ * T:(r_ + 1) * T, 1024 + h * 256:1024 + (h + 1) * 256].rearrange(
                        "(t p) c -> p t c", p=128)
                    dma("sp", VV[vi][:, i0:i0 + T // 128, 0:256], src, allT, ["VV%d" % vi])
            return vi

        def attend(kind, ki, vi, qn, qnk, qr, qrk, Tq, ktiles, scale, nv):
            n = len(ktiles)
            sbank = {}

            def issue_s(i):
                kt = ktiles[i]
                pi = 3 + ps_next(5)
                sbank[i] = pi
                if kind == "m":
                    mm(pi, 128, Tq, KT[ki][:, kt * 128:(kt + 1) * 128], qn, True, False, ["KT%d" % ki, qnk])
                    mm(pi, 128, Tq, KR[:64, kt * 128:(kt + 1) * 128], qr, False, True, ["KR", qrk])
                else:
                    mm(pi, 128, Tq, KT[ki][:, kt * 128:(kt + 1) * 128], qn, True, True, ["KT%d" % ki, qnk])
            for i in range(min(2, n)):
                issue_s(i)
            for i in range(n):
                kt = ktiles[i]
                pi = sbank[i]
                pt = rotv("PT", 3)
                act(PT[pt][:, :Tq], psb[pi][:, :Tq], AF.Exp, [PSK(pi)], ["PT%d" % pt], scale=scale)
                if i + 2 < n:
                    issue_s(i + 2)
                for c in range(nv):
                    mm(c, 128, Tq, VV[vi][:, kt, c * 128:(c + 1) * 128], PT[pt][:, :Tq], i == 0, i == n - 1,
                       ["VV%d" % vi, "PT%d" % pt])
                mm(2, 128, Tq, onesb[:], PT[pt][:, :Tq], i == 0, i == n - 1, ["onesb", "PT%d" % pt])

        for h in range(8):
            ki = load_head("m", h)
            vi = load_v("m", h)
            for bb in qblocks:
                t0, Tq = blocks[bb]
                qi = rotv("Q", 2)
                dma("sp", QN[qi][:, :Tq], qT[h * 192:h * 192 + 128, t0:t0 + Tq], [("qT", bb)], ["QN%d" % qi])
                dma("sp", QR[qi][:64, :Tq], qT[h * 192 + 128:h * 192 + 192, t0:t0 + Tq], [("qT", bb)], ["QR%d" % qi])
                ktiles = [0, 1] if bb == 0 else list(range(NKT))
                attend("m", ki, vi, QN[qi][:, :Tq], "QN%d" % qi, QR[qi][:64, :Tq], "QR%d" % qi, Tq, ktiles, 192.0 ** -0.5, 1)
                P.add("dve", lambda e, Tq=Tq: e.reciprocal(out=rl[:, :Tq], in_=psb[2][:, :Tq]), r=[PSK(2)], w=["rl"])
                oi = rotv("OB", 2)
                tt("dve", OB[oi][:, 0, :Tq], psb[0][:, :Tq], rl[:, :Tq], ALU.mult, [PSK(0), "rl"], ["OB%d" % oi])
                dma("pool", yaT[h * 128:(h + 1) * 128, t0:t0 + Tq], OB[oi][:, 0, :Tq], ["OB%d" % oi], [("yaT", bb)])

        for hd in range(4):
            vi = load_v("d", hd)
            kis = [load_head("d", hd * 2 + m_) for m_ in range(2)]
            for bb in qblocks:
                t0, Tq = blocks[bb]
                ktiles = [0, 1] if bb == 0 else list(range(NKT))
                for m_ in range(2):
                    qi = rotv("Q", 2)
                    j = hd * 2 + m_
                    dma("sp", QN[qi][:, :Tq], qdT[j * 128:(j + 1) * 128, t0:t0 + Tq], [("qdT", bb)], ["QN%d" % qi])
                    attend("d", kis[m_], vi, QN[qi][:, :Tq], "QN%d" % qi, None, None, Tq, ktiles, 128.0 ** -0.5, 2)
                    P.add("dve", lambda e, Tq=Tq: e.reciprocal(out=rl[:, :Tq], in_=psb[2][:, :Tq]), r=[PSK(2)], w=["rl"])
                    for c in range(2):
                        tt("dve", om[:, m_, c, :Tq], psb[c][:, :Tq], rl[:, :Tq], ALU.mult, [PSK(c), "rl"], ["om"])
                for c in range(2):
                    stt("dve", dd[:, c, :Tq], om[:, 1, c, :Tq], smalls[:, 5:6], om[:, 0, c, :Tq], ALU.mult, ALU.add,
                        ["om", "smalls"], ["dd"])
                    act(SQ[:, c, :Tq], dd[:, c, :Tq], AF.Square, ["dd"], ["SQd"])
                pi = 3 + ps_next(5)
                for c in range(2):
                    mm(pi, 128, Tq, onesb[:], SQ[:, c, :Tq], c == 0, c == 1, ["onesb", "SQd"])
                rsqrt_chain(rd[:, :Tq], "rdd", psb[pi][:, :Tq], PSK(pi), 1.0 / 256.0, EPS)
                oi = rotv("OB", 2)
                for c in range(2):
                    tt("dve", dd[:, c, :Tq], dd[:, c, :Tq], rd[:, :Tq], ALU.mult, ["dd", "rdd"], ["dd"])
                    act(OB[oi][:, c, :Tq], dd[:, c, :Tq], AF.Identity, ["dd", "smalls"], ["OB%d" % oi], scale=smalls[:, 6 + c:7 + c])
                dma("pool", ydT[hd * 256:(hd + 1) * 256, t0:t0 + Tq].rearrange("(c p) t -> p c t", p=128),
                    OB[oi][:, :, :Tq], ["OB%d" % oi], [("ydT", bb)])

    def ln_setup():
        v = {}
        v["yp"] = AFp.view("yp", 16, 512, sub=True)
        v["hc"] = [AFp.view("hc%d" % i, 512) for i in range(2)]
        v["sq"] = [AFp.view("lsq%d" % i, 512) for i in range(2)]
        v["mean"] = AFp.view("lmean", 512)
        v["rstd"] = AFp.view("lrstd", 512)
        v["nmr"] = AFp.view("lnmr", 512)
        v["ho"] = [AFp.view("lho%d" % i, 512) for i in range(2)]
        v["xo"] = [AB.view("lxo%d" % i, 512, at=NBF - 1024 + i * 512) for i in range(2)]
        return v

    def ln_chunk_in(v, dc, pi, T, b, t0, gsel, r_, lp):
        hi = rotv("hc", 2)
        dma("sp", v["hc"][hi][:, :T], hT[dc * 128:(dc + 1) * 128, t0:t0 + T], [("hT", b)], ["hc%d" % hi])
        stt("dve", v["yp"][:, dc, :T], psb[pi][:, :T], der[:, r_, lp, gsel, dc:dc + 1], v["hc"][hi][:, :T], ALU.mult, ALU.add,
            [PSK(pi), "der%d" % lp, "hc%d" % hi], [("yp", dc)])

    def ln_stats(v, dc, T):
        si = rotv("lsq", 2)
        act(v["sq"][si][:, :T], v["yp"][:, dc, :T], AF.Square, [("yp", dc)], ["lsq%d" % si])
        mm(6, 128, T, onesf[:], v["yp"][:, dc, :T], dc == 0, dc == 15, ["onesf", ("yp", dc)])
        mm(7, 128, T, onesf[:], v["sq"][si][:, :T], dc == 0, dc == 15, ["onesf", "lsq%d" % si])

    def ln_finish(v, T, b, t0, r_, l, which, final_out):
        mean, rstd, nmr = v["mean"], v["rstd"], v["nmr"]
        ts("dve", mean[:, :T], psb[6][:, :T], 1.0 / D, 0.0, ALU.mult, ALU.add, [PSK(6)], ["lmean"])
        tt("dve", nmr[:, :T], mean[:, :T], mean[:, :T], ALU.mult, ["lmean"], ["lnmr"])
        stt("dve", rstd[:, :T], psb[7][:, :T], 1.0 / D, nmr[:, :T], ALU.mult, ALU.subtract, [PSK(7), "lnmr"], ["lrstd"])
        rsqrt_chain(rstd[:, :T], "lrstd", rstd[:, :T], "lrstd", 1.0, EPS / (ALPHA * ALPHA))
        stt("dve", nmr[:, :T], mean[:, :T], -1.0, rstd[:, :T], ALU.mult, ALU.mult, ["lmean", "lrstd"], ["lnmr"])
        gcol, bcol = (VL1G, VL1B) if which == 1 else (VL2G, VL2B)
        for dc in range(16):
            yk = ("yp", dc)
            tt("dve", v["yp"][:, dc, :T], v["yp"][:, dc, :T], rstd[:, :T], ALU.mult, [yk, "lrstd"], [yk])
            tt("pool", v["yp"][:, dc, :T], v["yp"][:, dc, :T], nmr[:, :T], ALU.add, [yk, "lnmr"], [yk])
            hi = rotv("lho", 2)
            act(v["ho"][hi][:, :T], v["yp"][:, dc, :T], AF.Identity, [yk, "vecT"], ["lho%d" % hi],
                scale=vecT[:, gcol + dc:gcol + dc + 1], bias=vecT[:, bcol + dc:bcol + dc + 1])
            if final_out and b > 0:
                for tti in range(T // 128):
                    pi = ps_next(6)
                    P.add("pe", lambda e, pi=pi, hi=hi, tti=tti: e.transpose(out=psb[pi][:, :128],
                                                                             in_=v["ho"][hi][:, tti * 128:(tti + 1) * 128],
                                                                             identity=identf[:]),
                          r=["lho%d" % hi, "identf"], w=[PSK(pi)])
                    k = rotv("lsq", 2)
                    P.add("dve", lambda e, pi=pi, k=k: e.tensor_copy(out=v["sq"][k][:, :128], in_=psb[pi][:, :128]),
                          r=[PSK(pi)], w=["lsq%d" % k])
                    row = t0 - 128 + tti * 128
                    dma("pool", out[row:row + 128, dc * 128:(dc + 1) * 128], v["sq"][k][:, :128], ["lsq%d" % k], [("out", b)])
            if not final_out:
                dma("pool", hT[dc * 128:(dc + 1) * 128, t0:t0 + T], v["ho"][hi][:, :T], ["lho%d" % hi], [("hT", b)])
                xi = rotv("lxo", 2)
                act(v["xo"][xi][:, :T], v["yp"][:, dc, :T], AF.Identity, [yk, "fus"], ["lxo%d" % xi],
                    scale=fus[:, r_, 0, dc:dc + 1], bias=fus[:, r_, 1, dc:dc + 1])
                dma("pool", xmT[dc * 128:(dc + 1) * 128, t0:t0 + T], v["xo"][xi][:, :T], ["lxo%d" % xi], [("xmT", b)])

    def phase_C1(l, last):
        AB.off = W_END_B
        AFp.reset()
        v = ln_setup()
        yb = [AB.view("yb%d" % n, 8, 512) for n in range(3)]
        mg = AB.view("mg", 16, 512, sub=True)
        gat = [AB.view("gat%d" % i, 4, 512) for i in range(2)]
        acc = AFp.view("acc", 4, 512, sub=True)
        tmp = [AFp.view("ctmp%d" % i, 512) for i in range(2)]
        srcs = [(yaT, "yaT"), (ysT, "ysT"), (ydT, "ydT")]
        for b, (t0, T) in enumerate(blocks):
            if last and b == 0:
                continue
            r_ = 0 if b > 0 else 1
            fused_affine(r_, VL1G, VL1B, l % 2, DA2, DB2)
            for n in range(3):
                dma("sp", yb[n][:, :, :T], srcs[n][0][:, t0:t0 + T].rearrange("(kc p) t -> p kc t", p=128),
                    [(srcs[n][1], b)], ["yb%d" % n])
            for cg in range(4):
                for n in range(3):
                    wt, wk = wtile()
                    wv = wview(wk, 8, 512)
                    dma("sp", wv, wb_br[l][n * 1024:(n + 1) * 1024, cg * 512:(cg + 1) * 512].rearrange("(kc p) c -> p kc c", p=128),
                        WK("br", l), [wk])
                    gi = rotv("gat", 2)
                    dma("sp", gat[gi][:, :, :T],
                        gT[n * D + cg * 512:n * D + (cg + 1) * 512, t0:t0 + T].rearrange("(c p) t -> p c t", p=128),
                        [("gT", b)], ["gat%d" % gi])
                    for c in range(4):
                        dc = cg * 4 + c
                        pi = ps_next(6)
                        for kc in range(8):
                            mm(pi, 128, T, wv[:, kc, c * 128:(c + 1) * 128], yb[n][:, kc, :T], kc == 0, kc == 7, [wk, "yb%d" % n])
                        if n == 0:
                            tt("dve", acc[:, c, :T], psb[pi][:, :T], gat[gi][:, c, :T], ALU.mult, [PSK(pi), "gat%d" % gi], [("acc", c)])
                        else:
                            k = rotv("ctmp", 2)
                            tt("dve", tmp[k][:, :T], psb[pi][:, :T], gat[gi][:, c, :T], ALU.mult, [PSK(pi), "gat%d" % gi], ["ctmp%d" % k])
                            if n == 1:
                                tt("pool", acc[:, c, :T], acc[:, c, :T], tmp[k][:, :T], ALU.add, [("acc", c), "ctmp%d" % k], [("acc", c)])
                            else:
                                tt("pool", mg[:, dc, :T], acc[:, c, :T], tmp[k][:, :T], ALU.add, [("acc", c), "ctmp%d" % k], [("mg", dc)])
            pend = []

            def ev_o(c, m, pi):
                dc = c // 128
                ln_chunk_in(v, dc, pi, T, b, t0, DG1, r_, l % 2)
                pend.append(dc)
                if len(pend) > 1:
                    ln_stats(v, pend.pop(0), T)
            linear_fm(mg, "mg", 16, wb_out[l], WK("out", l), [(i * 128, 128) for i in range(16)], T, ev_o, pool=6)
            while pend:
                ln_stats(v, pend.pop(0), T)
            ln_finish(v, T, b, t0, r_, l, 1, False)

    def phase_C2(l, last):
        AB.off = X_END_B
        AFp.reset()
        v = ln_setup()
        hff = AB.view("hff", 44, 512, sub=True)
        sg = [AFp.view("sg%d" % i, 512) for i in range(2)]
        for b, (t0, T) in enumerate(blocks):
            if last and b == 0:
                continue
            r_ = 0 if b > 0 else 1
            if not last:
                fused_affine(r_, VL2G, VL2B, (l + 1) % 2, DA1, DB1)
            dma("sp", xm[:, :, :T], xmT[:, t0:t0 + T].rearrange("(kc p) t -> p kc t", p=128), [("xmT", b)], ["xm"])
            for g0 in range(0, FFN, 512):
                wtg, wkg = wtile()
                dma("sp", wtg[:, :, :], wb_gu[l][:, g0:g0 + 512].rearrange("(kc p) c -> p kc c", p=128), WK("gu", l), [wkg])
                wtu, wku = wtile()
                dma("sp", wtu[:, :, :], wb_gu[l][:, FFN + g0:FFN + g0 + 512].rearrange("(kc p) c -> p kc c", p=128), WK("gu", l), [wku])
                for c in range(4):
                    j = g0 // 128 + c
                    pg, pu = ps_next(6), ps_next(6)
                    for kc in range(16):
                        mm(pg, 128, T, wtg[:, kc, c * 128:(c + 1) * 128], xm[:, kc, :T], kc == 0, kc == 15, [wkg, "xm"])
                    for kc in range(16):
                        mm(pu, 128, T, wtu[:, kc, c * 128:(c + 1) * 128], xm[:, kc, :T], kc == 0, kc == 15, [wku, "xm"])
                    si = rotv("sg", 2)
                    act(sg[si][:, :T], psb[pg][:, :T], AF.Silu, [PSK(pg)], ["sg%d" % si])
                    tt("dve", hff[:, j, :T], sg[si][:, :T], psb[pu][:, :T], ALU.mult, ["sg%d" % si, PSK(pu)], [("hff", j)])
            pend = []
            for cg in range(4):
                banks = [ps_next(6) for _ in range(4)]
                for (k0, kn) in ((0, 16), (16, 16), (32, 12)):
                    wt, wk = wtile()
                    dma("sp", wt[:, :kn, :], wb_dn[l][k0 * 128:(k0 + kn) * 128, cg * 512:(cg + 1) * 512].rearrange(
                        "(kc p) c -> p kc c", p=128), WK("dn", l), [wk])
                    for c in range(4):
                        for kc in range(kn):
                            mm(banks[c], 128, T, wt[:, kc, c * 128:(c + 1) * 128], hff[:, k0 + kc, :T],
                               k0 + kc == 0, k0 + kc == 43, [wk, "hff"])
                for c in range(4):
                    dc = cg * 4 + c
                    ln_chunk_in(v, dc, banks[c], T, b, t0, DG2, r_, l % 2)
                    pend.append(dc)
                    if len(pend) > 1:
                        ln_stats(v, pend.pop(0), T)
            while pend:
                ln_stats(v, pend.pop(0), T)
            ln_finish(v, T, b, t0, r_, l, 2, last)

    def phase_in():
        AB.off = X_END_B
        AFp.reset()
        xt = [AFp.view("xt%d" % i, 2048) for i in range(2)]
        ho = [AFp.view("iho%d" % i, 512) for i in range(2)]
        xo = [AB.view("ixo%d" % i, 512) for i in range(2)]
        for b, (t0, T) in enumerate(blocks):
            r_ = 0 if b > 0 else 1
            for tti in range(T // 128):
                xi = rotv("xt", 2)
                dma("sp", xt[xi][:, :], xin[t0 + tti * 128:t0 + (tti + 1) * 128, :], [], ["xt%d" % xi])
                for dc in range(16):
                    pi = ps_next()
                    P.add("pe", lambda e, pi=pi, xi=xi, dc=dc: e.transpose(out=psb[pi][:, :128], in_=xt[xi][:, dc * 128:(dc + 1) * 128],
                                                                           identity=identf[:]),
                          r=["xt%d" % xi, "identf"], w=[PSK(pi)])
                    hi, oi = rotv("iho", 2), rotv("ixo", 2)
                    P.add("dve", lambda e, pi=pi, hi=hi: e.tensor_copy(out=ho[hi][:, :128], in_=psb[pi][:, :128]), r=[PSK(pi)], w=["iho%d" % hi])
                    c0 = t0 + tti * 128
                    dma("pool", hT[dc * 128:(dc + 1) * 128, c0:c0 + 128], ho[hi][:, :128], ["iho%d" % hi], [("hT", b)])
                    act(xo[oi][:, :128], psb[pi][:, :128], AF.Identity, [PSK(pi), "der0"], ["ixo%d" % oi],
                        scale=der[:, r_, 0, DA1, dc:dc + 1], bias=der[:, r_, 0, DB1, dc:dc + 1])
                    dma("pool", xmT[dc * 128:(dc + 1) * 128, c0:c0 + 128], xo[oi][:, :128], ["ixo%d" % oi], [("xmT", b)])

    gather_layer(0, ["ada", "in", "uq", "ukv"])
    gather_layer(0, ["br", "out", "gu", "dn"], background=True)
    layer_mod(0)
    layer_vectors(0)
    phase_in()
    for l in range(L):
        last = (l == L - 1)
        phase_A(l, last)
        if not last:
            gather_layer(l + 1, ["ada", "in", "uq", "ukv", "br", "out", "gu", "dn"], background=True)
        phase_B(l, last)
        if l == 0:
            bg_flush()
        phase_C1(l, last)
        if not last:
            bg_flush(("ada", l + 1))
            layer_mod(l + 1)
        phase_C2(l, last)
        if not last:
            bg_flush()
            layer_vectors(l + 1)
    for name, ap_ in dbg.items():
        src = {"hT": hT, "xmT": xmT, "qT": qT, "qdT": qdT, "gT": gT, "ysT": ysT, "yaT": yaT, "ydT": ydT}.get(name)
        if src is None:
            kind_, bi = name.split("_")
            src = {"exf": exf_loc, "ext": ext_loc, "exa": exf_a, "exb": exf_b, "eta": ext_all}[kind_][int(bi)].ap()
        keys = list(P.last_w.keys())
        n0 = src.shape[0]
        for r0 in range(0, n0, 1024):
            r1 = min(n0, r0 + 1024)
            dma("pool", ap_[r0:r1, :], src[r0:r1, :], keys, [])
    P.emit(sems)
    es.close()
    return nc, P


def _rope_tables(n_lat_total, dim, tok_idx):
    rows = n_lat_total // 64
    r = np.repeat(np.arange(rows, dtype=np.float32), 64)
    col = np.tile(np.arange(64, dtype=np.float32), rows)
    quarter = dim // 4
    inv = (np.float32(10000.0) ** (-np.arange(quarter, dtype=np.float32) / np.float32(quarter))).astype(np.float32)
    ar = r[:, None] * inv
    ac = col[:, None] * inv
    ang = np.concatenate([ar, ar, ac, ac], axis=-1)[tok_idx]
    return np.cos(ang).astype(np.float32).T, np.sin(ang).astype(np.float32).T


def _rot_mat(dim):
    q = dim // 4
    m = np.zeros((128, 128), np.float32)
    for a in range(2):
        for i in range(q):
            i0 = a * 2 * q + i
            i1 = a * 2 * q + q + i
            m[i1, i0] = -1.0
            m[i0, i1] = 1.0
    return m


_CACHE = {}


def run(inputs, L, debug=()):
    x = np.asarray(inputs["x"], np.float32)
    B, S, _ = x.shape
    half = S // 2
    NLB = half // 512
    key = (NLB, L, tuple(debug))
    if key not in _CACHE:
        _CACHE[key] = build(NLB, L, debug)
    nc, P = _CACHE[key]
    NTOK = 128 + half
    cmat = np.stack([np.eye(128, dtype=np.float32), _rot_mat(64), _rot_mat(128), np.ones((128, 128), np.float32)])
    names = ["ada_b", "mla_q_norm", "mla_kv_norm", "sgu_ln_g", "sgu_ln_b",
             "sgu_w", "diff_lam", "diff_subln", "ln1_g", "ln1_b", "ln2_g", "ln2_b"]
    shared = {n: np.ascontiguousarray(np.asarray(inputs[n], np.float32)[:L]) for n in names}
    shared["sgu_b"] = np.ascontiguousarray(np.asarray(inputs["sgu_b"], np.float32)[:L].reshape(L, 1024))
    shared["cmat"] = cmat
    shards = [dict() for _ in range(4)]
    for wn, (iname, K_, N_, R_) in WSPEC.items():
        w = np.asarray(inputs[iname], np.float32)[:L].reshape(L, K_ // (4 * R_), 4, R_, N_)
        for r4 in range(4):
            shards[r4][iname] = np.ascontiguousarray(w[:, :, r4].reshape(L, K_ // 4, N_))
    in_maps = []
    for core in range(8):
        b, p = core // 2, core % 2
        m = dict(shared)
        m.update(shards[core % 4])
        m["xin"] = np.concatenate([np.asarray(inputs["ctx"], np.float32)[b, p * 128:(p + 1) * 128],
                                   x[b, p * half:(p + 1) * half]], axis=0)
        m["cvec"] = np.stack([np.asarray(inputs["c"], np.float32)[b], np.asarray(inputs["c_ctx"], np.float32)])
        idx = np.arange(p * half, (p + 1) * half)
        for nm, dim in (("ropeM", 64), ("ropeD", 128)):
            cs, sn = _rope_tables(S, dim, idx)
            cs = np.concatenate([np.ones((dim, 128), np.float32), cs], axis=1)
            sn = np.concatenate([np.zeros((dim, 128), np.float32), sn], axis=1)
            m[nm] = np.ascontiguousarray(np.stack([cs, sn]))
        in_maps.append(m)
    res = run_bass_kernel_spmd(nc, in_maps, core_ids=list(range(8)))
    outp = np.empty((B, S, D), np.float32)
    for core in range(8):
        b, p = core // 2, core % 2
        outp[b, p * half:(p + 1) * half] = res.results[core]["out"]
    return outp, res


def kernel(**inputs):
    outp, _ = run(inputs, 4)
    return outp
```
